# Optimizing a Trainium2 kernel written in Bass

```python
import math
import jax, jax.numpy as jnp
from jax import lax
import numpy as np

D_MODEL = 1024
BATCH = 4
SEQ = 8192
DEPTH = 2

PLE_DIM = 256
D_MIX_SSD = D_MODEL
D_MIX_CONV = D_MODEL
D_MIX = D_MIX_SSD + D_MIX_CONV
SSD_HEAD_DIM = 64
SSD_HEADS = D_MIX_SSD // SSD_HEAD_DIM
SSD_GROUPS = 2
HEADS_PER_GROUP = SSD_HEADS // SSD_GROUPS
D_STATE = 64
SSD_CONV_W = 4
CHUNK = 128
SSD_CONV_DIM = D_MIX_SSD + 2 * SSD_GROUPS * D_STATE
SC_CONV_W = 3
IN_SPLITS = [SSD_CONV_DIM, SSD_HEADS, D_MIX_SSD, D_MIX_CONV, D_MIX_CONV, D_MIX_CONV, D_MIX_CONV]
IN_COLS = sum(IN_SPLITS)
EPS = 1e-6

kernel_name = "hybrid_ssd_shortconv_parallel_heads"


def rmsnorm(x, w):
    xf = x.astype(jnp.float32)
    xf = xf * lax.rsqrt(jnp.mean(xf * xf, axis=-1, keepdims=True) + EPS)
    return (xf * w.astype(jnp.float32)).astype(x.dtype)


def causal_dwconv(u, w, b=None):
    k_w = w.shape[0]
    length = u.shape[1]
    up = jnp.pad(u, ((0, 0), (k_w - 1, 0), (0, 0)))
    y = up[:, 0:length] * w[0]
    for k in range(1, k_w):
        y = y + up[:, k:k + length] * w[k]
    if b is not None:
        y = y + b
    return y


def segsum(a):
    t = a.shape[-1]
    cs = jnp.cumsum(a, axis=-1)
    diff = cs[..., :, None] - cs[..., None, :]
    mask = jnp.tril(jnp.ones((t, t), dtype=bool))
    return jnp.where(mask, diff, -jnp.inf)


def ssd_chunked(x, dt, a, b_mat, c_mat, d_skip):
    bsz, length, _, _ = x.shape
    nc = length // CHUNK
    g, r, p, n = SSD_GROUPS, HEADS_PER_GROUP, SSD_HEAD_DIM, D_STATE
    xd = (x * dt[..., None]).reshape(bsz, nc, CHUNK, g, r, p)
    bc = b_mat.reshape(bsz, nc, CHUNK, g, n)
    cc = c_mat.reshape(bsz, nc, CHUNK, g, n)
    adt = (dt * a).reshape(bsz, nc, CHUNK, g, r).transpose(0, 3, 4, 1, 2)
    a_cs = jnp.cumsum(adt, axis=-1)
    decay_in = jnp.exp(segsum(adt))
    cb = jnp.einsum("bclgn,bcsgn->bgcls", cc, bc)
    y_diag = jnp.einsum("bgrcls,bcsgrp->bclgrp", cb[:, :, None] * decay_in, xd)
    decay_states = jnp.exp(a_cs[..., -1:] - a_cs).transpose(0, 3, 4, 1, 2)
    states = jnp.einsum("bclgn,bclgrp->bcgrpn", bc, xd * decay_states[..., None])
    init = jnp.zeros_like(states[:, :1])
    states = jnp.concatenate([init, states], axis=1)
    totals = jnp.pad(a_cs[..., -1], ((0, 0), (0, 0), (0, 0), (1, 0)))
    decay_chunk = jnp.exp(segsum(totals))
    new_states = jnp.einsum("bgrzc,bcgrpn->bzgrpn", decay_chunk, states)
    prev_states = new_states[:, :-1]
    decay_out = jnp.exp(a_cs).transpose(0, 3, 4, 1, 2)
    y_off = jnp.einsum("bclgn,bcgrpn->bclgrp", cc, prev_states) * decay_out[..., None]
    y = (y_diag + y_off).reshape(bsz, length, SSD_HEADS, p)
    return y + d_skip[:, None] * x


def hybrid_layer(x, p_i, norm_pre, norm_post, w_in, ssd_conv_w, ssd_conv_b, dt_bias,
                 a_log, d_skip, ssd_norm, sc_conv_w, w_out, w_ple_gate, w_ple_proj):
    bsz, length, _ = x.shape
    h = rmsnorm(x, norm_pre)
    proj = h @ w_in
    idx = list(np.cumsum(IN_SPLITS)[:-1])
    xbc, dt_raw, z_ssd, sc_h, sc_b, sc_c, z_sc = jnp.split(proj, idx, axis=-1)

    xbc = jax.nn.silu(causal_dwconv(xbc, ssd_conv_w, ssd_conv_b))
    xs, bm, cm = jnp.split(xbc, [D_MIX_SSD, D_MIX_SSD + SSD_GROUPS * D_STATE], axis=-1)
    xs = xs.reshape(bsz, length, SSD_HEADS, SSD_HEAD_DIM)
    bm = bm.reshape(bsz, length, SSD_GROUPS, D_STATE)
    cm = cm.reshape(bsz, length, SSD_GROUPS, D_STATE)
    dt = jax.nn.softplus(dt_raw + dt_bias)
    a = -jnp.exp(a_log)
    y_ssd = ssd_chunked(xs, dt, a, bm, cm, d_skip).reshape(bsz, length, D_MIX_SSD)
    yz = (y_ssd * jax.nn.silu(z_ssd)).reshape(bsz, length, SSD_GROUPS, D_MIX_SSD // SSD_GROUPS)
    y_ssd = rmsnorm(yz, ssd_norm.reshape(SSD_GROUPS, -1)).reshape(bsz, length, D_MIX_SSD)

    v = causal_dwconv(sc_c * sc_h, sc_conv_w)
    y_sc = sc_b * v * jax.nn.silu(z_sc)

    mix = jnp.concatenate([y_ssd, y_sc], axis=-1) @ w_out
    x = x + rmsnorm(mix, norm_post)

    gate = jax.nn.sigmoid(x @ w_ple_gate)
    return x + gate * (p_i @ w_ple_proj)


def setup_inputs(seed: int = 0) -> dict:
    key = jax.random.key(seed)
    ks = jax.random.split(key, 16)
    f32 = jnp.float32
    x = jax.random.normal(ks[0], (BATCH, SEQ, D_MODEL), f32)
    p = jax.random.normal(ks[1], (DEPTH, BATCH, SEQ, PLE_DIM), f32)
    norm_pre = 1.0 + 0.05 * jax.random.normal(ks[2], (DEPTH, D_MODEL), f32)
    norm_post = 1.0 + 0.05 * jax.random.normal(ks[3], (DEPTH, D_MODEL), f32)
    w_in = jax.random.normal(ks[4], (DEPTH, D_MODEL, IN_COLS), f32) * D_MODEL ** -0.5
    ssd_conv_w = jax.random.normal(ks[5], (DEPTH, SSD_CONV_W, SSD_CONV_DIM), f32) * SSD_CONV_W ** -0.5
    ssd_conv_b = 0.02 * jax.random.normal(ks[6], (DEPTH, SSD_CONV_DIM), f32)
    dt0 = jnp.exp(jax.random.uniform(ks[7], (DEPTH, SSD_HEADS), f32,
                                     math.log(1e-3), math.log(1e-1)))
    dt_bias = dt0 + jnp.log(-jnp.expm1(-dt0))
    a_log = jnp.log(jax.random.uniform(ks[8], (DEPTH, SSD_HEADS), f32, 1.0, 16.0))
    d_skip = 1.0 + 0.05 * jax.random.normal(ks[9], (DEPTH, SSD_HEADS), f32)
    ssd_norm = 1.0 + 0.05 * jax.random.normal(ks[10], (DEPTH, D_MIX_SSD), f32)
    sc_conv_w = jax.random.normal(ks[11], (DEPTH, SC_CONV_W, D_MIX_CONV), f32) * SC_CONV_W ** -0.5
    w_out = jax.random.normal(ks[12], (DEPTH, D_MIX, D_MODEL), f32) * D_MIX ** -0.5
    w_ple_gate = jax.random.normal(ks[13], (DEPTH, D_MODEL, D_MODEL), f32) * D_MODEL ** -0.5
    w_ple_proj = jax.random.normal(ks[14], (DEPTH, PLE_DIM, D_MODEL), f32) * (0.5 * PLE_DIM ** -0.5)
    return {"x": x, "p": p, "norm_pre": norm_pre, "norm_post": norm_post, "w_in": w_in,
            "ssd_conv_w": ssd_conv_w, "ssd_conv_b": ssd_conv_b, "dt_bias": dt_bias,
            "a_log": a_log, "d_skip": d_skip, "ssd_norm": ssd_norm, "sc_conv_w": sc_conv_w,
            "w_out": w_out, "w_ple_gate": w_ple_gate, "w_ple_proj": w_ple_proj}


def reference(x, p, norm_pre, norm_post, w_in, ssd_conv_w, ssd_conv_b, dt_bias, a_log,
              d_skip, ssd_norm, sc_conv_w, w_out, w_ple_gate, w_ple_proj):
    for i in range(DEPTH):
        x = hybrid_layer(x, p[i], norm_pre[i], norm_post[i], w_in[i], ssd_conv_w[i],
                         ssd_conv_b[i], dt_bias[i], a_log[i], d_skip[i], ssd_norm[i],
                         sc_conv_w[i], w_out[i], w_ple_gate[i], w_ple_proj[i])
    return x
```

```python
import os
import sys
import numpy as np
import ml_dtypes
import concourse.bass as bass
import concourse.mybir as mybir
from concourse.bass_utils import run_bass_kernel_spmd

F32 = mybir.dt.float32
BF16 = mybir.dt.bfloat16
AF = mybir.ActivationFunctionType
ALU = mybir.AluOpType

D = 1024
NCOLS = 6416
C1 = 2320
C2 = 4096
EPS = 1e-6
MT = 512
NEG = -30000.0
CUR = {"banks": None, "line": None}
STRICT_SAME_ENGINE = True


class Prog:
    ENG = ("sp", "pe", "act", "dve", "pool")

    def __init__(self):
        self.ops = []
        self.nbanks = 8
        self.ps_next = 0
        self.held = set()

    def op(self, eng, fn, r=(), w=()):
        self.ops.append(dict(eng=eng, fn=fn, r=tuple(r), w=tuple(w), dma=None, line=sys._getframe(1).f_lineno))

    def dma(self, fn, r=(), w=(), sem=None):
        self.ops.append(dict(eng="sp", fn=fn, r=tuple(r), w=tuple(w), dma=sem, line=sys._getframe(1).f_lineno))

    def bank(self, hold=False):
        for _ in range(self.nbanks):
            b = self.ps_next
            self.ps_next = (self.ps_next + 1) % self.nbanks
            if b not in self.held:
                break
        else:
            raise RuntimeError("all PSUM banks held")
        if hold:
            self.held.add(b)
        self.ops.append(dict(eng=None, fence=b))
        return b

    def free(self, *banks):
        for b in banks:
            self.held.discard(b)

    def finalize(self, nc, stack):
        ops = self.ops
        lim = int(os.environ.get("KLIMIT", "0"))
        if lim:
            kept, n = [], 0
            for o in ops:
                if o["eng"] is not None:
                    n += 1
                    if n > lim:
                        break
                kept.append(o)
            ops = kept
            print("KLIMIT: last op line", ops[-1].get("line"), ops[-1]["eng"])
        last_w = {}
        readers = {}
        touched = {b: [] for b in range(self.nbanks)}
        bank_last = {b: {} for b in range(self.nbanks)}
        fence = {b: [] for b in range(self.nbanks)}
        real = []
        last_eng = {}
        last_dma = {}
        bar = set()
        for o in ops:
            if o["eng"] is None:
                if o.get("barrier"):
                    bar = set(last_eng.values()) | set(last_dma.values())
                    continue
                b = o["fence"]
                fence[b] = touched[b]
                touched[b] = []
                continue
            i = len(real)
            real.append(o)
            deps = set(bar)
            raw = set(bar)
            if o["dma"] is not None:
                last_dma[o["dma"]] = i
            else:
                last_eng[o["eng"]] = i
            for k in o["r"]:
                if k in last_w:
                    deps.add(last_w[k]); raw.add(last_w[k])
            for k in o["w"]:
                if k in last_w:
                    deps.add(last_w[k])
                for rr in readers.get(k, ()):
                    deps.add(rr)
            banks = set(k[1] for k in o["r"] + o["w"] if isinstance(k, tuple) and k[0] == "ps")
            for b in banks:
                for e2, j2 in bank_last[b].items():
                    if e2 != o["eng"]:
                        deps.add(j2)
                bank_last[b][o["eng"]] = i
            for k in o["r"]:
                readers.setdefault(k, []).append(i)
            for k in o["w"]:
                last_w[k] = i
                readers[k] = []
            deps.discard(i)
            o["deps"] = deps
            o["raw"] = raw
        need = [False] * len(real)
        for i, o in enumerate(real):
            best = {}
            for j in o["deps"]:
                pj = real[j]
                if pj["dma"] is not None:
                    kk = ("dma", pj["dma"])
                else:
                    if pj["eng"] == o["eng"]:
                        if o["eng"] == "pe":
                            continue
                        if j not in o["raw"] and not STRICT_SAME_ENGINE:
                            continue
                    kk = ("eng", pj["eng"])
                if j > best.get(kk, -1):
                    best[kk] = j
            keep = set(best.values())
            for j in keep:
                if real[j]["dma"] is None:
                    need[j] = True
            o["deps"] = keep
        cnt = {e: 0 for e in self.ENG}
        dcnt = {}
        for i, o in enumerate(real):
            if o["dma"] is not None:
                dcnt[o["dma"]] = dcnt.get(o["dma"], 0) + 16
                o["sig"] = (("dma", o["dma"]), dcnt[o["dma"]])
            elif need[i]:
                cnt[o["eng"]] += 1
                o["sig"] = (("eng", o["eng"]), cnt[o["eng"]])
            else:
                o["sig"] = None
        known = {e: {} for e in self.ENG}
        for o in real:
            want = {}
            for j in o["deps"]:
                s, v = real[j]["sig"]
                if v > want.get(s, 0):
                    want[s] = v
            kn = known[o["eng"]]
            waits = []
            for s, v in want.items():
                if kn.get(s, 0) >= v:
                    continue
                kn[s] = v
                waits.append((s, v))
            o["waits"] = waits
        sems = {}
        for e in self.ENG:
            if e != "sp":
                sems[("eng", e)] = stack.enter_context(nc.semaphore("s_" + e))
        for k in dcnt:
            sems[("dma", k)] = stack.enter_context(nc.semaphore("d_" + str(k)))
        self.stats = dict(n=len(real), sig=dict(cnt), dma=len(dcnt))
        final = [(("dma", k), v) for k, v in dcnt.items()]
        final += [(("eng", e), c) for e, c in cnt.items() if e != "sp" and c > 0]

        def emit(engname, e):
            for o in real:
                if o["eng"] != engname:
                    continue
                for s, v in o["waits"]:
                    e.wait_ge(sems[s], v)
                CUR["banks"] = set(k[1] for k in o["r"] + o["w"] if isinstance(k, tuple) and k[0] == "ps")
                CUR["line"] = o.get("line")
                ins = o["fn"](e)
                CUR["banks"] = None
                if o["sig"] is not None:
                    s, v = o["sig"]
                    ins.then_inc(sems[s], 16 if s[0] == "dma" else 1)
            if engname == "sp":
                for s, v in final:
                    e.wait_ge(sems[s], v)

        with nc.Block() as blk:
            blk.sync(lambda e: emit("sp", e))
            blk.tensor(lambda e: emit("pe", e))
            blk.scalar(lambda e: emit("act", e))
            blk.vector(lambda e: emit("dve", e))
            blk.gpsimd(lambda e: emit("pool", e))


def build_program(S, n_layers=2, debug=None):
    import contextlib
    assert S % MT == 0
    NM = S // MT
    NCH = S // 128
    nc = bass.Bass("TRN2", target_bir_lowering=False)
    stack = contextlib.ExitStack()
    stack.enter_context(nc.allow_low_precision(reason="bf16 matmul operands, fp32 accumulate"))
    P = Prog()

    def din(name, shape, dt=F32):
        return nc.dram_tensor(name, list(shape), dt, kind="ExternalInput").ap()

    L = n_layers
    x_in = din("x", [S, D])
    p_in = din("p", [L, S, 256])
    w_in = din("w_in", [L, D, NCOLS])
    w_out = din("w_out", [L, 2048, D])
    w_gate = din("w_gate", [L, D, D])
    w_ple = din("w_ple", [L, 256, D])
    npre_d = din("npre", [L, 128, 8])
    snorm_d = din("snorm", [L, 128, 8])
    npost_d = din("npost", [L, D])
    cw_d = din("cw", [L, 128, 40])
    cb_d = din("cb", [L, 128, 10])
    scw_d = din("scw", [L, 128, 24])
    dsk_d = din("dsk", [L, 128, 8])
    dtb_d = din("dtb", [L, 16])
    alog_d = din("alog", [L, 16])
    ident_d = din("ident", [128, 128], BF16)
    tri_d = din("tri", [128, 128])
    triu_d = din("triu", [128, 128], BF16)
    negm_d = din("negm", [128, 512], BF16)
    ones_d = din("ones", [128, 128])
    out_d = nc.dram_tensor("out", [S, D], F32, kind="ExternalOutput").ap()
    x1_d = nc.dram_tensor("x1_scr", [S, D], F32, kind="Internal").ap()
    yn_d = nc.dram_tensor("yn_scr", [NCH, 128, 1024], BF16, kind="Internal").ap()
    hT_d = nc.dram_tensor("hT_scr", [NM, 128, 8 * MT], BF16, kind="Internal").ap()
    dbg_outs = {}

    def sb(name, shape, dt=F32):
        return nc.alloc_sbuf_tensor("sb_" + name, list(shape), dt)

    arena = {}

    def arena_init():
        nbytes = (nc.sbuf_bytes_remaining - 64) // 64 * 64
        arena["t"] = nc.alloc_sbuf_tensor("arena", [128, nbytes // 2], BF16)
        arena["n"] = nbytes
        arena["off"] = 0

    def ab(name, shape, dt=F32):
        esz = 4 if dt == F32 else 2
        n = 1
        for s_ in shape[1:]:
            n *= s_
        nb = (n * esz + 31) // 32 * 32
        off = arena["off"]
        assert off + nb <= arena["n"], ("arena overflow", name, off, nb, arena["n"])
        arena["off"] = off + nb
        a = arena["t"][0:shape[0], off // 2:off // 2 + (n * esz) // 2]
        if dt == F32:
            a = a.bitcast(F32)
        if len(shape) == 3:
            a = a.rearrange("p (a b) -> p a b", a=shape[1])
        return a

    pbank = [nc.alloc_psum_tensor("pb%d" % i, [128, 512], F32) for i in range(8)]

    def pf(b):
        assert CUR["banks"] is None or b in CUR["banks"], ("undeclared PSUM bank access", b, CUR["line"])
        return pbank[b][:, :]

    def pbf(b):
        assert CUR["banks"] is None or b in CUR["banks"], ("undeclared PSUM bank access", b, CUR["line"])
        return pbank[b][:, :].bitcast(BF16)

    ident = sb("ident", [128, 128], BF16)
    tri = sb("tri", [128, 128])
    triub = sb("triub", [128, 128], BF16)
    negm = sb("negm", [128, 512], BF16)
    ones = sb("ones", [128, 128])
    neghalf = sb("neghalf", [128, 2])
    for t, d_, nm in ((ident, ident_d, "ident"), (tri, tri_d, "tri"), (triub, triu_d, "triub"),
                      (negm, negm_d, "negm"), (ones, ones_d, "ones")):
        P.dma(lambda e, t=t, d_=d_: e.dma_start(out=t[:, :], in_=d_[:, :]), w=[nm], sem="c_" + nm)
    P.op("pool", lambda e: e.memset(neghalf[:, :], -0.5), w=["neghalf"])

    xt = [sb("xt%d" % i, [128, D]) for i in range(2)]
    junk = sb("junk", [128, D], BF16)
    hb = [sb("hb%d" % i, [128, D], BF16) for i in range(2)]
    sm = [sb("sm%d" % i, [128, 8]) for i in range(4)]
    npre_t = sb("npre_t", [128, 8])
    arena_init()
    cur = {}

    def barrier():
        P.ops.append(dict(eng=None, barrier=True))

    cnt = dict(x=0, stg=0, wop=0, sm=0)

    def load_w(dst_ap, dst_key, src_ap, ncols, scale_ap=None, scale_key=None, cscale=None):
        stg = cur["stg"]
        i = cnt["stg"]; cnt["stg"] += 1
        s = i % len(stg)
        dst_key = (dst_key, i)
        P.dma(lambda e: e.dma_start(out=stg[s][:, 0:ncols], in_=src_ap), w=[("stg", s)], sem="stg%d" % s)
        eng = ("act", "dve", "pool")[cnt["wop"] % 3]; cnt["wop"] += 1
        rk = [("stg", s)] + ([scale_key] if scale_key else [])
        src = stg[s][:, 0:ncols]
        if scale_ap is None and cscale is None:
            if eng == "act":
                P.op("act", lambda e: e.activation(out=dst_ap, in_=src, func=AF.Copy), r=rk, w=[dst_key])
            else:
                P.op(eng, lambda e: e.tensor_copy(out=dst_ap, in_=src), r=rk, w=[dst_key])
        elif scale_ap is None:
            if eng == "act":
                P.op("act", lambda e: e.activation(out=dst_ap, in_=src, func=AF.Copy, scale=float(cscale)), r=rk, w=[dst_key])
            else:
                P.op(eng, lambda e: e.tensor_scalar(out=dst_ap, in0=src, scalar1=float(cscale), scalar2=0.0,
                                                    op0=ALU.mult, op1=ALU.add), r=rk, w=[dst_key])
        else:
            if eng == "act":
                P.op("act", lambda e: e.activation(out=dst_ap, in_=src, func=AF.Copy, scale=scale_ap), r=rk, w=[dst_key])
            else:
                P.op(eng, lambda e: e.tensor_scalar(out=dst_ap, in0=src, scalar1=scale_ap, scalar2=0.0,
                                                    op0=ALU.mult, op1=ALU.add), r=rk, w=[dst_key])

    def small_dma(dst_t, dst_key, src_ap, sem):
        P.dma(lambda e: e.dma_start(out=dst_t, in_=src_ap), w=[dst_key], sem=sem)

    def phaseA(xsrc, m):
        for c in range(cur["MT"] // 128):
            phaseA_chunk(xsrc, m, c)

    def phaseA_chunk(xsrc, m, c):
        phaseA_ew(xsrc, m, c)
        phaseA_pe(m, c)

    pa_state = {}

    def phaseA_ew(xsrc, m, c):
        t0 = m * cur["MT"] + c * 128
        i = cnt["x"]; cnt["x"] += 1
        s = i % 2
        q = cnt["sm"] % 4; cnt["sm"] += 1
        pa_state[(m, c)] = s
        P.dma(lambda e, s=s, t0=t0: e.dma_start(out=xt[s][:, :], in_=xsrc[t0:t0 + 128, :]),
              w=[("xt", s)], sem="xt%d" % s)
        P.op("act", lambda e, s=s, q=q: e.activation(out=junk[:, :], in_=xt[s][:, :], func=AF.Square,
                                                      accum_out=sm[q][:, 0:1]),
             r=[("xt", s)], w=["junk", ("sm", q, 0)])
        P.op("pool", lambda e, q=q: e.tensor_scalar(out=sm[q][:, 1:2], in0=sm[q][:, 0:1], scalar1=1.0 / D,
                                                     scalar2=EPS, op0=ALU.mult, op1=ALU.add),
             r=[("sm", q, 0)], w=[("sm", q, 1)])
        P.op("pool", lambda e, q=q: e.tensor_tensor(out=sm[q][:, 2:3], in0=sm[q][:, 1:2], in1=neghalf[:, 0:1],
                                                     op=ALU.pow),
             r=[("sm", q, 1), "neghalf"], w=[("sm", q, 2)])
        P.op("dve", lambda e, s=s, q=q: e.tensor_scalar(out=hb[s][:, :], in0=xt[s][:, :], scalar1=sm[q][:, 2:3],
                                                         scalar2=None, op0=ALU.mult),
             r=[("xt", s), ("sm", q, 2)], w=[("hb", s)])

    def phaseA_pe(m, c):
        hT = cur["hT"]
        par = m % len(hT)
        s = pa_state.pop((m, c))
        b = P.bank()
        for kc in range(8):
            P.op("pe", lambda e, s=s, kc=kc, b=b: e.transpose(out=pbf(b)[:, kc * 128:(kc + 1) * 128],
                                                              in_=hb[s][:, kc * 128:(kc + 1) * 128],
                                                              identity=ident[:, :]),
                 r=[("hb", s), "ident"], w=[("ps", b)])
        P.op("act", lambda e, b=b, par=par, c=c: e.activation(
            out=hT[par][:, :, c * 128:(c + 1) * 128],
            in_=pbf(b).rearrange("p (k t) -> p k t", k=8), func=AF.Copy),
            r=[("ps", b)], w=[("hT", par, c)])

    hT_keys = lambda par: [("hT", par, c) for c in range(cur["MT"] // 128)]

    def pass1(l, xsrc):
        barrier()
        arena["off"] = 0
        def load_consts_p1():
            small_dma(npre_t[:, :], "npre_t", npre_d[l], "c_npre")
            small_dma(cw_t[:, :], "cw_t", cw_d[l], "c_cw")
            small_dma(cb_t[:, :], "cb_t", cb_d[l], "c_cb")
            small_dma(dsk_t[:, :], "dsk_t", dsk_d[l], "c_dsk")
            small_dma(dtb_b[:, :], "dtb_b", dtb_d[l:l + 1, :].partition_broadcast(128), "c_dtb")
            small_dma(a_b[:, :], "a_b", alog_d[l:l + 1, :].partition_broadcast(128), "c_alog")
            P.op("act", lambda e: e.activation(out=a_b[:, :], in_=a_b[:, :], func=AF.Exp), r=["a_b"], w=["a_b"])
            P.op("dve", lambda e: e.tensor_scalar(out=a_b[:, :], in0=a_b[:, :], scalar1=-1.0, scalar2=None, op0=ALU.mult),
                 r=["a_b"], w=["a_b"])
            P.op("dve", lambda e: e.tensor_tensor(
                out=dgx[:, :, :], in0=ident[:, :].unsqueeze(1).broadcast_to([128, 40, 128]),
                in1=cw_t[:, :].unsqueeze(2).broadcast_to([128, 40, 128]), op=ALU.mult),
                r=["ident", "cw_t"], w=["dgx"])
            P.op("dve", lambda e: e.tensor_tensor(
                out=dskd[:, :, :], in0=ident[:, :].unsqueeze(1).broadcast_to([128, 8, 128]),
                in1=dsk_t[:, :].unsqueeze(2).broadcast_to([128, 8, 128]), op=ALU.mult),
                r=["ident", "dsk_t"], w=["dskd"])
            for kc in range(8):
                for c0, c1 in ((0, 1024), (1024, 2048), (2048, C1)):
                    load_w(w1[:, kc, c0:c1], "w1", w_in[l, kc * 128:(kc + 1) * 128, c0:c1], c1 - c0,
                           scale_ap=npre_t[:, kc:kc + 1], scale_key="npre_t")

        w1 = ab("w1", [128, 8, C1], BF16)
        dgx = ab("dgx", [128, 40, 128], BF16)
        dskd = ab("dskd", [128, 8, 128], BF16)
        cw_t = ab("cw_t", [128, 40])
        cb_t = ab("cb_t", [128, 10])
        dsk_t = ab("dsk_t", [128, 8])
        dtb_b = ab("dtb_b", [128, 16])
        a_b = ab("a_b", [128, 16])
        off_work = arena["off"]
        stg = [ab("stg%d" % i, [128, 1024]) for i in range(8)]
        cur["stg"] = stg
        load_consts_p1()
        barrier()
        arena["off"] = off_work
        hT = [ab("hT%d" % i, [128, 8, MT], BF16) for i in range(2)]
        ub = ab("ub", [128, 10, MT + 4], BF16)
        xbcT = [ab("xbcT%d" % i, [128, 10, MT], BF16) for i in range(2)]
        th_sb = ab("th_sb", [128, D], BF16)
        g_sb = [ab("g_sb%d" % i, [128, D], BF16) for i in range(2)]
        D4 = [ab("dts4_%d" % i, [128, 8, 64]) for i in range(2)]
        A2 = [ab("adt24_%d" % i, [128, 2, 64], BF16) for i in range(2)]
        E4 = [ab("etot4_%d" % i, [128, 4, 8]) for i in range(2)]
        Xs = [ab("Xs%d" % i, [128, 32, 128], BF16) for i in range(2)]
        Eb = [ab("Eb%d" % i, [128, 16, 128], BF16) for i in range(2)]
        Mt = [ab("Mt%d" % i, [128, 16, 128], BF16) for i in range(2)]
        btok = [ab("btok%d" % i, [128, 128], BF16) for i in range(2)]
        cbs = [ab("cbs%d" % i, [128, 256], BF16) for i in range(2)]
        xd = [ab("xd%d" % i, [128, D], BF16) for i in range(2)]
        xdd = [ab("xdd%d" % i, [128, D], BF16) for i in range(2)]
        Sst = ab("Sst", [128, 512])
        Sbf = ab("Sbf", [128, 512], BF16)
        ty = [ab("ty%d" % i, [128, D]) for i in range(2)]
        yz = [ab("yz%d" % i, [128, D]) for i in range(2)]
        ssg = [ab("ssg%d" % i, [128, 8]) for i in range(2)]
        yn = [ab("yn%d" % i, [128, D], BF16) for i in range(2)]
        ynT = [ab("ynT%d" % i, [128, 8, 128], BF16) for i in range(2)]
        cur["hT"] = hT
        cur["stg"] = stg
        cur["MT"] = MT

        P.op("pool", lambda e: e.memset(ub[:, :, :], 0.0), w=[("ub", j) for j in range(10)])
        P.op("pool", lambda e: e.memset(Sst[:, :], 0.0), w=["Sst"])
        P.op("pool", lambda e: e.memset(Sbf[:, :], 0.0), w=["Sbf"])

        def b1_halo():
            P.op("dve", lambda e: e.tensor_copy(out=ub[:, :, 0:3], in_=ub[:, :, MT:MT + 3]),
                 r=[("ub", j) for j in range(10)], w=[("ub", j) for j in range(10)])

        b1_bank = {}

        def b1_proj(m, j):
            par = m % 2
            b = P.bank(hold=True)
            b1_bank[(m, j)] = b
            for kc in range(8):
                P.op("pe", lambda e, kc=kc: e.matmul(
                    pf(b), lhsT=w1[:, kc, j * 128:(j + 1) * 128], rhs=hT[par][:, kc, :],
                    start=(kc == 0), stop=(kc == 7)),
                    r=["w1"] + hT_keys(par), w=[("ps", b)])
            P.op("act", lambda e: e.activation(out=ub[:, j, 3:MT + 3], in_=pf(b), func=AF.Copy),
                 r=[("ps", b)], w=[("ub", j)])
            P.free(b)

        def b1_conv(m, j):
            par = m % 2
            b2 = P.bank()
            for k in range(4):
                P.op("pe", lambda e, k=k: e.matmul(
                    pf(b2), lhsT=dgx[:, j * 4 + k, :], rhs=ub[:, j, k:k + MT], start=(k == 0), stop=(k == 3)),
                    r=["dgx", ("ub", j)], w=[("ps", b2)])
            P.op("act", lambda e: e.activation(
                out=xbcT[par][:, j, :], in_=pf(b2), func=AF.Silu, bias=cb_t[:, j:j + 1]),
                r=[("ps", b2), "cb_t"], w=[("xbcT", par, j)])

        def b1_step(m, k):
            if k == 0:
                b1_halo()
            if k < 10:
                b1_proj(m, k)
            if 2 <= k < 12:
                b1_conv(m, k - 2)

        def ctx(m, c):
            par = m % 2
            ch = m * 4 + c
            s = ch % 2
            return par, ch, s, slice(c * 128, (c + 1) * 128), [("xbcT", par, j) for j in range(10)]

        def s0(m):
            par = m % 2
            d = D4[par]
            dk = lambda i: ("dts4", par, i)
            bs = P.bank()
            for c in range(4):
                ts = slice(c * 128, (c + 1) * 128)
                for kc in range(8):
                    P.op("pe", lambda e, kc=kc, c=c, ts=ts: e.matmul(
                        pf(bs)[:, c * 16:(c + 1) * 16], lhsT=hT[par][:, kc, ts], rhs=w1[:, kc, 1280:1296],
                        start=(kc == 0), stop=(kc == 7)),
                        r=["w1", ("hT", par, c)], w=[("ps", bs)])
            P.op("dve", lambda e: e.tensor_tensor(
                out=d[:, 0, :].rearrange("p (c h) -> p c h", c=4), in0=pf(bs)[:, 0:64].rearrange("p (c h) -> p c h", c=4),
                in1=dtb_b[:, :].unsqueeze(1).broadcast_to([128, 4, 16]), op=ALU.add),
                r=[("ps", bs), "dtb_b"], w=[dk(0)])
            P.op("dve", lambda e: e.tensor_scalar(out=d[:, 1, :], in0=d[:, 0, :], scalar1=-1.0, scalar2=None, op0=ALU.mult),
                 r=[dk(0)], w=[dk(1)])
            P.op("dve", lambda e: e.tensor_tensor(out=d[:, 1, :], in0=d[:, 1, :], in1=d[:, 0, :], op=ALU.min),
                 r=[dk(0), dk(1)], w=[dk(1)])
            P.op("act", lambda e: e.activation(out=d[:, 1, :], in_=d[:, 1, :], func=AF.Exp), r=[dk(1)], w=[dk(1)])
            P.op("act", lambda e: e.activation(out=d[:, 2, :], in_=d[:, 1, :], func=AF.Ln, bias=1.0), r=[dk(1)], w=[dk(2)])
            P.op("dve", lambda e: e.scalar_tensor_tensor(out=d[:, 3, :], in0=d[:, 0, :], scalar=0.0, in1=d[:, 2, :],
                                                         op0=ALU.max, op1=ALU.add), r=[dk(0), dk(2)], w=[dk(3)])
            P.op("dve", lambda e: e.tensor_tensor(
                out=d[:, 4, :].rearrange("p (c h) -> p c h", c=4), in0=d[:, 3, :].rearrange("p (c h) -> p c h", c=4),
                in1=a_b[:, :].unsqueeze(1).broadcast_to([128, 4, 16]), op=ALU.mult),
                r=[dk(3), "a_b"], w=[dk(4)])
            P.op("dve", lambda e: e.tensor_copy(out=A2[par][:, 0, :], in_=d[:, 4, :]), r=[dk(4)], w=[("adt2", par, 0)])
            P.op("dve", lambda e: e.tensor_tensor(out=A2[par][:, 1, :], in0=d[:, 4, :], in1=A2[par][:, 0, :], op=ALU.subtract),
                 r=[dk(4), ("adt2", par, 0)], w=[("adt2", par, 1)])

        def s0b(m):
            par = m % 2
            d = D4[par]
            dk = lambda i: ("dts4", par, i)
            b2 = P.bank()
            P.op("pe", lambda e: e.matmul(pf(b2)[:, 0:64], lhsT=tri[:, :], rhs=d[:, 4, :], start=True, stop=True),
                 r=["tri", dk(4)], w=[("ps", b2, "cs")])
            P.op("pe", lambda e: e.matmul(pf(b2)[:, 64:128], lhsT=ones[:, :], rhs=d[:, 4, :], start=True, stop=True),
                 r=["ones", dk(4)], w=[("ps", b2, "tot")])
            P.op("act", lambda e: e.activation(out=d[:, 5, :], in_=pf(b2)[:, 0:64], func=AF.Exp),
                 r=[("ps", b2, "cs")], w=[dk(5)])
            P.op("dve", lambda e: e.tensor_copy(out=d[:, 6, :], in_=pf(b2)[:, 0:64]), r=[("ps", b2, "cs")], w=[dk(6)])
            P.op("dve", lambda e: e.tensor_tensor(out=d[:, 6, :], in0=pf(b2)[:, 64:128], in1=d[:, 6, :], op=ALU.subtract),
                 r=[("ps", b2, "tot"), dk(6)], w=[dk(6)])
            P.op("act", lambda e: e.activation(out=d[:, 7, :], in_=d[:, 6, :], func=AF.Exp), r=[dk(6)], w=[dk(7)])
            P.op("act", lambda e: e.activation(
                out=E4[par][0:64, :, :], in_=pf(b2)[0:64, 64:128].rearrange("p (c h) -> p c h", c=4)[:, :, 0:8], func=AF.Exp),
                r=[("ps", b2, "tot")], w=[("etot", par, 0)])
            P.op("act", lambda e: e.activation(
                out=E4[par][64:128, :, :], in_=pf(b2)[64:128, 64:128].rearrange("p (c h) -> p c h", c=4)[:, :, 8:16], func=AF.Exp),
                r=[("ps", b2, "tot")], w=[("etot", par, 1)])

        def s1(m, c):
            par, ch, s, ts, xk = ctx(m, c)
            bz = [P.bank(), P.bank()]
            for nb in range(2):
                for kc in range(8):
                    P.op("pe", lambda e, nb=nb, kc=kc: e.matmul(
                        pf(bz[nb]), lhsT=hT[par][:, kc, ts], rhs=w1[:, kc, 1296 + nb * 512:1296 + (nb + 1) * 512],
                        start=(kc == 0), stop=(kc == 7)),
                        r=["w1", ("hT", par, c)], w=[("ps", bz[nb])])
            for nb in range(2):
                sl = slice(nb * 512, (nb + 1) * 512)
                P.op("act", lambda e, nb=nb, sl=sl: e.activation(out=g_sb[s][:, sl], in_=pf(bz[nb]), func=AF.Silu),
                     r=[("ps", bz[nb])], w=[("g", s, nb)])

        def sX(m, c):
            par, ch, s, ts, xk = ctx(m, c)
            P.op("pool", lambda e: e.tensor_tensor(
                out=Xs[s][:, 0:16, :], in0=tri[:, :].unsqueeze(1).broadcast_to([128, 16, 128]),
                in1=A2[par][:, 0, c * 16:(c + 1) * 16].unsqueeze(2).broadcast_to([128, 16, 128]), op=ALU.mult),
                r=["tri", ("adt2", par, 0)], w=[("Xs", s, 0)])
            P.op("dve", lambda e: e.tensor_tensor(
                out=Xs[s][:, 16:32, :], in0=tri[:, :].unsqueeze(1).broadcast_to([128, 16, 128]),
                in1=A2[par][:, 1, c * 16:(c + 1) * 16].unsqueeze(2).broadcast_to([128, 16, 128]), op=ALU.mult),
                r=["tri", ("adt2", par, 1)], w=[("Xs", s, 1)])

        def s2a(m, c):
            par, ch, s, ts, xk = ctx(m, c)
            d = D4[par][:, :, c * 16:(c + 1) * 16]
            dk = lambda i: ("dts4", par, i)
            bt = P.bank()
            for j in range(8):
                P.op("pe", lambda e, j=j: e.transpose(out=pbf(bt)[:, j * 128:(j + 1) * 128], in_=xbcT[par][:, j, ts],
                                                      identity=ident[:, :]),
                     r=[xk[j], "ident"], w=[("ps", bt)])
            bb = P.bank()
            P.op("pe", lambda e: e.transpose(out=pbf(bb)[:, 0:128], in_=xbcT[par][:, 8, ts], identity=ident[:, :]),
                 r=[xk[8], "ident"], w=[("ps", bb)])
            bc = [P.bank(), P.bank()]
            for g in range(2):
                gs = slice(g * 64, (g + 1) * 64)
                P.op("pe", lambda e, g=g, gs=gs: e.matmul(
                    pf(bc[g])[:, 0:128], lhsT=xbcT[par][gs, 8, ts], rhs=xbcT[par][gs, 9, ts],
                    start=True, stop=True), r=[xk[8], xk[9]], w=[("ps", bc[g])])
            P.op("dve", lambda e: e.tensor_tensor(
                out=xd[s][:, :].rearrange("p (h q) -> p h q", h=16), in0=pbf(bt).rearrange("p (h q) -> p h q", h=16),
                in1=d[:, 3, :].unsqueeze(2).broadcast_to([128, 16, 64]), op=ALU.mult),
                r=[("ps", bt), dk(3)], w=[("xd", s)])
            P.op("act", lambda e: e.activation(out=btok[s][:, :], in_=pbf(bb)[:, 0:128], func=AF.Copy),
                 r=[("ps", bb)], w=[("btok", s)])
            for g in range(2):
                P.op("dve", lambda e, g=g: e.tensor_copy(out=cbs[s][:, g * 128:(g + 1) * 128], in_=pf(bc[g])[:, 0:128]),
                     r=[("ps", bc[g])], w=[("cbs", s, g)])

        def s2b(m, c):
            par, ch, s, ts, xk = ctx(m, c)
            d = D4[par][:, :, c * 16:(c + 1) * 16]
            dk = lambda i: ("dts4", par, i)
            for q in range(4):
                bq = P.bank()
                P.op("pe", lambda e, bq=bq, q=q: e.matmul(
                    pf(bq), lhsT=triub[:, :], rhs=Xs[s][:, 4 * q:4 * q + 4, :].rearrange("p h l -> p (h l)"),
                    start=True, stop=False), r=["triub", ("Xs", s, 0)], w=[("ps", bq)])
                P.op("pe", lambda e, bq=bq, q=q: e.matmul(
                    pf(bq), lhsT=triub[:, :], rhs=Xs[s][:, 16 + 4 * q:16 + 4 * q + 4, :].rearrange("p h l -> p (h l)"),
                    start=False, stop=False), r=["triub", ("Xs", s, 1)], w=[("ps", bq)])
                P.op("pe", lambda e, bq=bq: e.matmul(pf(bq), lhsT=ident[:, :], rhs=negm[:, :], start=False, stop=True),
                     r=["ident", "negm"], w=[("ps", bq)])
                P.op("act", lambda e, bq=bq, q=q: e.activation(
                    out=Eb[s][:, 4 * q:4 * q + 4, :].rearrange("p h l -> p (h l)"), in_=pf(bq), func=AF.Exp),
                    r=[("ps", bq)], w=[("Eb", s, q)])

        def s2c(m, c):
            par, ch, s, ts, xk = ctx(m, c)
            d = D4[par][:, :, c * 16:(c + 1) * 16]
            dk = lambda i: ("dts4", par, i)
            P.op("pool", lambda e: e.tensor_tensor(
                out=xdd[s][:, :].rearrange("p (h q) -> p h q", h=16), in0=xd[s][:, :].rearrange("p (h q) -> p h q", h=16),
                in1=d[:, 7, :].unsqueeze(2).broadcast_to([128, 16, 64]), op=ALU.mult),
                r=[("xd", s), dk(7)], w=[("xdd", s)])
            P.op("pool", lambda e: e.tensor_tensor(
                out=Mt[s][:, :, :].rearrange("p (g r) l -> p g r l", g=2),
                in0=Eb[s][:, :, :].rearrange("p (g r) l -> p g r l", g=2),
                in1=cbs[s][:, :].rearrange("p (g l) -> p g l", g=2).unsqueeze(2).broadcast_to([128, 2, 8, 128]),
                op=ALU.mult), r=[("Eb", s, q) for q in range(4)] + [("cbs", s, 0), ("cbs", s, 1)], w=[("Mt", s)])

        def s3(m, c):
            par, ch, s, ts, xk = ctx(m, c)
            d = D4[par][:, :, c * 16:(c + 1) * 16]
            dk = lambda i: ("dts4", par, i)
            bn = P.bank()
            for g in range(2):
                gs = slice(g * 64, (g + 1) * 64)
                P.op("pe", lambda e, g=g, gs=gs: e.matmul(
                    pf(bn)[gs, :], lhsT=btok[s][:, gs], rhs=xdd[s][:, g * 512:(g + 1) * 512], start=True, stop=True),
                    r=[("btok", s), ("xdd", s)], w=[("ps", bn, g)])
            P.op("dve", lambda e: e.tensor_tensor(
                out=Sst[:, :].rearrange("p (r q) -> p r q", r=8), in0=Sst[:, :].rearrange("p (r q) -> p r q", r=8),
                in1=E4[par][:, c, :].unsqueeze(2).broadcast_to([128, 8, 64]), op=ALU.mult),
                r=["Sst", ("etot", par, 0), ("etot", par, 1)], w=["Sst"])
            P.op("dve", lambda e: e.tensor_tensor(out=Sst[:, :], in0=pf(bn), in1=Sst[:, :], op=ALU.add),
                 r=["Sst", ("ps", bn, 0), ("ps", bn, 1)], w=["Sst"])
            by = [P.bank(), P.bank()]
            for nb in range(2):
                for jj in range(4):
                    j = nb * 4 + jj
                    P.op("pe", lambda e, nb=nb, jj=jj, j=j: e.matmul(
                        pf(by[nb])[:, jj * 128:(jj + 1) * 128], lhsT=xbcT[par][:, j, ts], rhs=dskd[:, j, :],
                        start=(jj == 0), stop=False, skip_group_check=True), r=[xk[j], "dskd"], w=[("ps", by[nb])])
                for hh in range(8):
                    h = nb * 8 + hh
                    P.op("pe", lambda e, nb=nb, hh=hh, h=h: e.matmul(
                        pf(by[nb])[:, hh * 64:(hh + 1) * 64], lhsT=Mt[s][:, h, :], rhs=xd[s][:, h * 64:(h + 1) * 64],
                        start=False, stop=(hh == 7), skip_group_check=True), r=[("Mt", s), ("xd", s)], w=[("ps", by[nb])])
            bo = [P.bank(), P.bank()]
            for g in range(2):
                gs = slice(g * 64, (g + 1) * 64)
                P.op("pe", lambda e, g=g, gs=gs: e.matmul(
                    pf(bo[g]), lhsT=xbcT[par][gs, 9, ts], rhs=Sbf[gs, :], start=True, stop=True),
                    r=[xk[9], "Sbf"], w=[("ps", bo[g])])
            P.op("act", lambda e: e.activation(out=Sbf[:, :], in_=Sst[:, :], func=AF.Copy), r=["Sst"], w=["Sbf"])
            for g in range(2):
                sl = slice(g * 512, (g + 1) * 512)
                P.op("dve", lambda e, g=g, sl=sl: e.tensor_tensor(
                    out=ty[s][:, sl].rearrange("p (r q) -> p r q", r=8), in0=pf(bo[g]).rearrange("p (r q) -> p r q", r=8),
                    in1=d[:, 5, g * 8:(g + 1) * 8].unsqueeze(2).broadcast_to([128, 8, 64]), op=ALU.mult),
                    r=[("ps", bo[g]), dk(5)], w=[("ty", s, g)])
                P.op("dve", lambda e, g=g, sl=sl: e.tensor_tensor(out=ty[s][:, sl], in0=pf(by[g]), in1=ty[s][:, sl], op=ALU.add),
                     r=[("ps", by[g]), ("ty", s, g)], w=[("ty", s, g)])
                P.op("pool", lambda e, g=g, sl=sl: e.tensor_tensor(out=yz[s][:, sl], in0=ty[s][:, sl], in1=g_sb[s][:, sl], op=ALU.mult),
                     r=[("ty", s, g), ("g", s, g)], w=[("yz", s, g)])
                P.op("act", lambda e, g=g, sl=sl: e.activation(out=junk[:, sl], in_=yz[s][:, sl], func=AF.Square,
                                                              accum_out=ssg[s][:, g:g + 1]),
                     r=[("yz", s, g)], w=["junk", ("ssg", s, g)])

        def s3b(m, c):
            par, ch, s, ts, xk = ctx(m, c)
            q = ssg[s]
            P.op("pool", lambda e: e.tensor_scalar(out=q[:, 2:4], in0=q[:, 0:2], scalar1=1.0 / 512.0,
                                                   scalar2=EPS, op0=ALU.mult, op1=ALU.add),
                 r=[("ssg", s, 0), ("ssg", s, 1)], w=[("ssg", s, 2)])
            P.op("pool", lambda e: e.tensor_tensor(out=q[:, 6:8], in0=q[:, 2:4], in1=neghalf[:, 0:2], op=ALU.pow),
                 r=[("ssg", s, 2), "neghalf"], w=[("ssg", s, 6)])
            for g in range(2):
                sl = slice(g * 512, (g + 1) * 512)
                P.op("act", lambda e, g=g, sl=sl: e.activation(out=yn[s][:, sl], in_=yz[s][:, sl], func=AF.Copy,
                                                              scale=q[:, 6 + g:7 + g]),
                     r=[("yz", s, g), ("ssg", s, 6)], w=[("yn", s, g)])

        def s4(m, c):
            par, ch, s, ts, xk = ctx(m, c)
            bt2 = P.bank()
            for j in range(8):
                P.op("pe", lambda e, j=j: e.transpose(out=pbf(bt2)[:, j * 128:(j + 1) * 128],
                                                      in_=yn[s][:, j * 128:(j + 1) * 128], identity=ident[:, :]),
                     r=[("yn", s, j // 4), "ident"], w=[("ps", bt2)])
            P.op("dve", lambda e: e.tensor_copy(out=ynT[s][:, :, :].rearrange("p k t -> p (k t)"), in_=pbf(bt2)),
                 r=[("ps", bt2)], w=[("ynT", s)])
            P.dma(lambda e: e.dma_start(out=yn_d[ch], in_=ynT[s][:, :, :].rearrange("p k t -> p (k t)")),
                  r=[("ynT", s)], w=[("yn_d", ch)], sem="yno%d" % s)

        def hT_store(mm):
            par = mm % 2
            P.dma(lambda e: e.dma_start(out=hT_d[mm], in_=hT[par][:, :, :].rearrange("p k t -> p (k t)")),
                  r=[("hT", par, c) for c in range(4)], w=[("hT_d", mm)], sem="hTo%d" % par)

        phaseA(xsrc, 0)
        hT_store(0)
        s0(0)
        for k in range(12):
            b1_step(0, k)
        s0b(0)
        if NM > 1:
            phaseA(xsrc, 1)
            hT_store(1)
        pend4 = []
        for m in range(NM):
            nxt = m + 1 < NM
            tq = list(range(12)) if nxt else []
            if nxt:
                s0(m + 1)

            def T(n):
                for _ in range(n):
                    if tq:
                        b1_step(m + 1, tq.pop(0))
            for pi, pr in enumerate(((0, 1), (2, 3))):
                pa = m + 2 < NM
                for c in pr:
                    s1(m, c)
                    sX(m, c)
                while pend4:
                    s4(*pend4.pop(0))
                if pa:
                    for c in pr:
                        phaseA_ew(xsrc, m + 2, c)
                T(1)
                for c in pr:
                    s2a(m, c)
                if nxt and pi == 0:
                    s0b(m + 1)
                T(1)
                for c in pr:
                    s2b(m, c)
                for c in pr:
                    s2c(m, c)
                T(1)
                for c in pr:
                    s3(m, c)
                T(1)
                for c in pr:
                    s3b(m, c)
                if pa:
                    for c in pr:
                        phaseA_pe(m + 2, c)
                    if pi == 1:
                        hT_store(m + 2)
                for c in pr:
                    pend4.append((m, c))
                T(2 if pi == 0 else 12)

        while pend4:
            s4(*pend4.pop(0))

    def pass2(l, xsrc, xdst):
        barrier()
        arena["off"] = 0
        MT2 = 256
        CPM = MT2 // 128
        NM2 = S // MT2
        cur["MT"] = MT2
        w2 = ab("w2", [128, 8, C2], BF16)
        wo = ab("wo", [128, 16, D], BF16)
        wg = ab("wg", [128, 8, D], BF16)
        wp = ab("wp", [128, 2, D], BF16)
        dgs = ab("dgs", [128, 24, 128], BF16)
        scw_t = ab("scw_t", [128, 24])
        sn_t = ab("sn_t", [128, 8])
        npost_b = ab("npost_b", [128, D])
        off_work = arena["off"]
        stg = [ab("stg%d" % i, [128, 1024]) for i in range(8)]
        cur["stg"] = stg

        small_dma(npre_t[:, :], "npre_t", npre_d[l], "c_npre")
        small_dma(scw_t[:, :], "scw_t", scw_d[l], "c_scw")
        small_dma(sn_t[:, :], "sn_t", snorm_d[l], "c_sn")
        small_dma(npost_b[:, :], "npost_b", npost_d[l:l + 1, :].partition_broadcast(128), "c_npost")
        P.op("dve", lambda e: e.tensor_tensor(
            out=dgs[:, :, :], in0=ident[:, :].unsqueeze(1).broadcast_to([128, 24, 128]),
            in1=scw_t[:, :].unsqueeze(2).broadcast_to([128, 24, 128]), op=ALU.mult),
            r=["ident", "scw_t"], w=["dgs"])
        for kc in range(8):
            for q in range(4):
                load_w(w2[:, kc, q * 1024:(q + 1) * 1024], "w2",
                       w_in[l, kc * 128:(kc + 1) * 128, C1 + q * 1024:C1 + (q + 1) * 1024], 1024,
                       scale_ap=npre_t[:, kc:kc + 1], scale_key="npre_t")
        for kc in range(16):
            if kc < 8:
                load_w(wo[:, kc, :], "wo", w_out[l, kc * 128:(kc + 1) * 128, :], 1024,
                       scale_ap=sn_t[:, kc:kc + 1], scale_key="sn_t")
            else:
                load_w(wo[:, kc, :], "wo", w_out[l, kc * 128:(kc + 1) * 128, :], 1024)
        for kc in range(8):
            load_w(wg[:, kc, :], "wg", w_gate[l, kc * 128:(kc + 1) * 128, :], 1024)
        for kc in range(2):
            load_w(wp[:, kc, :], "wp", w_ple[l, kc * 128:(kc + 1) * 128, :], 1024, cscale=0.5)
        barrier()
        arena["off"] = off_work
        hT = [ab("hT%d" % i, [128, 8, MT2], BF16) for i in range(2)]
        cur["hT"] = hT
        x1bs = [ab("x1b%d" % i, [128, D], BF16) for i in range(2)]
        chb = ab("chb", [128, 8, MT2 + 4], BF16)
        hs = ab("hs", [128, MT2])
        zs = ab("zs", [128, MT2])
        tb = [ab("tb%d" % i, [128, MT2]) for i in range(3)]
        yscT = [ab("yscT%d" % i, [128, 8, MT2], BF16) for i in range(2)]
        ynTi = [ab("ynTi%d" % i, [128, 8, 128], BF16) for i in range(2)]
        xr = [ab("xr%d" % i, [128, D]) for i in range(2)]
        pt = [ab("pt%d" % i, [128, 256]) for i in range(2)]
        x1s = [ab("x1s%d" % i, [128, D]) for i in range(2)]
        x1T = [ab("x1T%d" % i, [128, 8, 128], BF16) for i in range(2)]
        pbb = ab("pbb", [128, 256], BF16)
        pT = [ab("pT%d" % i, [128, 2, 128], BF16) for i in range(2)]
        th2 = ab("th2", [128, D])
        ss2 = [ab("ss2_%d" % i, [128, 8]) for i in range(2)]
        P.op("pool", lambda e: e.memset(chb[:, :, :], 0.0), w=[("chb", j) for j in range(8)])
        print("pass2 arena used", arena["off"], "of", arena["n"])

        def c2_loads(ch):
            s = ch % 2
            t0 = ch * 128
            P.dma(lambda e: e.dma_start(out=ynTi[s][:, :, :].rearrange("p k t -> p (k t)"), in_=yn_d[ch]),
                  r=[("yn_d", ch)], w=[("ynTi", s)], sem="yni%d" % s)
            P.dma(lambda e: e.dma_start(out=xr[s][:, :], in_=xsrc[t0:t0 + 128, :]), w=[("xr", s)], sem="xr%d" % s)
            P.dma(lambda e: e.dma_start(out=pt[s][:, :], in_=p_in[l, t0:t0 + 128, :]), w=[("pt", s)], sem="pt%d" % s)

        st2 = {}

        def c2_ptrans(ch):
            s = ch % 2
            P.op("act", lambda e: e.activation(out=pbb[:, :], in_=pt[s][:, :], func=AF.Copy), r=[("pt", s)], w=["pbb"])
            bp = P.bank()
            for kc in range(2):
                P.op("pe", lambda e, kc=kc: e.transpose(out=pbf(bp)[:, kc * 128:(kc + 1) * 128],
                                                        in_=pbb[:, kc * 128:(kc + 1) * 128], identity=ident[:, :]),
                     r=["pbb", "ident"], w=[("ps", bp)])
            P.op("dve", lambda e: e.tensor_copy(out=pT[s][:, :, :].rearrange("p k t -> p (k t)"), in_=pbf(bp)[:, 0:256]),
                 r=[("ps", bp)], w=[("pT", s)])

        def c2_outproj(m, c):
            ch = m * CPM + c
            s = ch % 2
            mp = m % 2
            ts = slice(c * 128, (c + 1) * 128)
            bm = [P.bank(hold=True), P.bank(hold=True)]
            st2[ch] = bm
            for nb in range(2):
                for kc in range(16):
                    if kc < 8:
                        lt = ynTi[s][:, kc, :]
                        rk = [("ynTi", s)]
                    else:
                        lt = yscT[mp][:, kc - 8, ts]
                        rk = [("yscT", mp, kc - 8)]
                    P.op("pe", lambda e, nb=nb, kc=kc, lt=lt: e.matmul(
                        pf(bm[nb]), lhsT=lt, rhs=wo[:, kc, nb * 512:(nb + 1) * 512], start=(kc == 0), stop=(kc == 15)),
                        r=rk + ["wo"], w=[("ps", bm[nb])])
                P.op("act", lambda e, nb=nb, s=s: e.activation(out=junk[:, nb * 512:(nb + 1) * 512], in_=pf(bm[nb]),
                                                          func=AF.Square, accum_out=ss2[s][:, nb:nb + 1]),
                     r=[("ps", bm[nb])], w=["junk", ("ss2", s, nb)])

        def c2_chainA(m, c):
            ch = m * CPM + c
            s = ch % 2
            bm = st2[ch]
            q2 = ss2[s]
            P.op("pool", lambda e: e.tensor_tensor(out=q2[:, 2:3], in0=q2[:, 0:1], in1=q2[:, 1:2], op=ALU.add),
                 r=[("ss2", s, 0), ("ss2", s, 1)], w=[("ss2", s, 2)])
            P.op("pool", lambda e: e.tensor_scalar(out=q2[:, 3:4], in0=q2[:, 2:3], scalar1=1.0 / D, scalar2=EPS,
                                                   op0=ALU.mult, op1=ALU.add), r=[("ss2", s, 2)], w=[("ss2", s, 3)])
            P.op("pool", lambda e: e.tensor_tensor(out=q2[:, 4:5], in0=q2[:, 3:4], in1=neghalf[:, 0:1], op=ALU.pow),
                 r=[("ss2", s, 3), "neghalf"], w=[("ss2", s, 4)])
            x1 = x1s[s]
            x1b = x1bs[s]
            for nb in range(2):
                sl = slice(nb * 512, (nb + 1) * 512)
                P.op("dve", lambda e, nb=nb, sl=sl: e.scalar_tensor_tensor(
                    out=x1[:, sl], in0=pf(bm[nb]), scalar=q2[:, 4:5], in1=npost_b[:, sl], op0=ALU.mult, op1=ALU.mult),
                    r=[("ps", bm[nb]), ("ss2", s, 4), "npost_b"], w=[("x1s", s, nb)])
            P.free(*bm)
            P.op("pool", lambda e: e.tensor_tensor(out=x1[:, :], in0=x1[:, :], in1=xr[s][:, :], op=ALU.add),
                 r=[("x1s", s, 0), ("x1s", s, 1), ("xr", s)], w=[("x1s", s, 0), ("x1s", s, 1)])
            P.op("act", lambda e: e.activation(out=x1b[:, :], in_=x1[:, :], func=AF.Copy),
                 r=[("x1s", s, 0), ("x1s", s, 1)], w=[("x1b", s)])

        def c2_chainA2(m, c):
            ch = m * CPM + c
            s = ch % 2
            x1b = x1bs[s]
            bt = P.bank()
            for kc in range(8):
                P.op("pe", lambda e, kc=kc: e.transpose(out=pbf(bt)[:, kc * 128:(kc + 1) * 128],
                                                        in_=x1b[:, kc * 128:(kc + 1) * 128], identity=ident[:, :]),
                     r=[("x1b", s), "ident"], w=[("ps", bt)])
            P.op("dve", lambda e: e.tensor_copy(out=x1T[s][:, :, :].rearrange("p k t -> p (k t)"), in_=pbf(bt)),
                 r=[("ps", bt)], w=[("x1T", s)])

        def c2_gate(m, c):
            ch = m * CPM + c
            s = ch % 2
            t0 = ch * 128
            x1 = x1s[s]
            bg = [P.bank(), P.bank()]
            bq = [P.bank(), P.bank()]
            for nb in range(2):
                for kc in range(8):
                    P.op("pe", lambda e, nb=nb, kc=kc: e.matmul(
                        pf(bg[nb]), lhsT=x1T[s][:, kc, :], rhs=wg[:, kc, nb * 512:(nb + 1) * 512],
                        start=(kc == 0), stop=(kc == 7)), r=[("x1T", s), "wg"], w=[("ps", bg[nb])])
                for kc in range(2):
                    P.op("pe", lambda e, nb=nb, kc=kc: e.matmul(
                        pf(bq[nb]), lhsT=pT[s][:, kc, :], rhs=wp[:, kc, nb * 512:(nb + 1) * 512],
                        start=(kc == 0), stop=(kc == 1)), r=[("pT", s), "wp"], w=[("ps", bq[nb])])
            for nb in range(2):
                sl = slice(nb * 512, (nb + 1) * 512)
                P.op("act", lambda e, nb=nb, sl=sl: e.activation(out=th2[:, sl], in_=pf(bg[nb]), func=AF.Tanh, scale=0.5),
                     r=[("ps", bg[nb])], w=[("th2", nb)])
                P.op("dve", lambda e, nb=nb, sl=sl: e.scalar_tensor_tensor(
                    out=th2[:, sl], in0=th2[:, sl], scalar=1.0, in1=pf(bq[nb]), op0=ALU.add, op1=ALU.mult),
                    r=[("th2", nb), ("ps", bq[nb])], w=[("th2", nb)])
            P.op("pool", lambda e: e.tensor_tensor(out=x1[:, :], in0=th2[:, :], in1=x1[:, :], op=ALU.add),
                 r=[("th2", 0), ("th2", 1), ("x1s", s, 0), ("x1s", s, 1)], w=[("x1s", s, 0), ("x1s", s, 1)])
            P.dma(lambda e: e.dma_start(out=xdst[t0:t0 + 128, :], in_=x1[:, :]),
                  r=[("x1s", s, 0), ("x1s", s, 1)], w=[("xdst", l, ch)], sem="xo%d" % s)

        def b2_halo():
            P.op("dve", lambda e: e.tensor_copy(out=chb[:, :, 0:2], in_=chb[:, :, MT2:MT2 + 2]),
                 r=[("chb", j) for j in range(8)], w=[("chb", j) for j in range(8)])

        def b2_proj(m, j):
            mp = m % 2
            tbs = tb[j % 3]

            def proj(off):
                b = P.bank()
                for kc in range(8):
                    P.op("pe", lambda e, b=b, kc=kc: e.matmul(
                        pf(b)[:, 0:MT2], lhsT=w2[:, kc, off + j * 128:off + (j + 1) * 128], rhs=hT[mp][:, kc, :],
                        start=(kc == 0), stop=(kc == 7)), r=["w2"] + hT_keys(mp), w=[("ps", b)])
                return b
            b = proj(0)
            P.op("act", lambda e, b=b: e.activation(out=hs[:, :], in_=pf(b)[:, 0:MT2], func=AF.Copy), r=[("ps", b)], w=["hs"])
            b = proj(2048)
            P.op("dve", lambda e, b=b: e.tensor_tensor(out=chb[:, j, 2:MT2 + 2], in0=pf(b)[:, 0:MT2], in1=hs[:, :], op=ALU.mult),
                 r=[("ps", b), "hs"], w=[("chb", j)])
            b = proj(3072)
            P.op("act", lambda e, b=b: e.activation(out=zs[:, :], in_=pf(b)[:, 0:MT2], func=AF.Silu), r=[("ps", b)], w=["zs"])
            b = proj(1024)
            P.op("dve", lambda e, b=b: e.tensor_tensor(out=tbs[:, :], in0=pf(b)[:, 0:MT2], in1=zs[:, :], op=ALU.mult),
                 r=[("ps", b), "zs"], w=[("tb", j % 3)])

        def b2_conv(m, j):
            mp = m % 2
            tbs = tb[j % 3]
            b = P.bank()
            for k in range(3):
                P.op("pe", lambda e, b=b, k=k: e.matmul(
                    pf(b)[:, 0:MT2], lhsT=dgs[:, j * 3 + k, :], rhs=chb[:, j, k:k + MT2], start=(k == 0), stop=(k == 2)),
                    r=["dgs", ("chb", j)], w=[("ps", b)])
            P.op("dve", lambda e, b=b: e.tensor_tensor(out=yscT[mp][:, j, :], in0=pf(b)[:, 0:MT2], in1=tbs[:, :], op=ALU.mult),
                 r=[("ps", b), ("tb", j % 3)], w=[("yscT", mp, j)])

        def b2_step(m, k):
            if k == 0:
                b2_halo()
            if k < 8:
                b2_proj(m, k)
            if 2 <= k < 10:
                b2_conv(m, k - 2)

        def hT_load(m2):
            p2 = m2 % 2
            half = m2 % 2
            srcv = hT_d[m2 // 2].rearrange("p (k t) -> p k t", k=8)[:, :, half * MT2:(half + 1) * MT2]
            P.dma(lambda e: e.dma_start(out=hT[p2][:, :, :], in_=srcv),
                  r=[("hT_d", m2 // 2)], w=[("hT", p2, c) for c in range(CPM)], sem="hTi%d" % p2)

        hT_load(0)
        for k in range(10):
            b2_step(0, k)
        if NM2 > 1:
            hT_load(1)
        c2_loads(0)
        for m in range(NM2):
            ch0 = m * CPM
            nxt = m + 1 < NM2
            pa = m + 2 < NM2
            tq = list(range(10)) if nxt else []

            def T(n):
                for _ in range(n):
                    if tq:
                        b2_step(m + 1, tq.pop(0))
            if pa:
                hT_load(m + 2)
            if ch0 + 1 < NCH:
                c2_loads(ch0 + 1)
            c2_ptrans(ch0)
            c2_outproj(m, 0)
            c2_chainA(m, 0)
            T(2)
            c2_chainA2(m, 0)
            if ch0 + 2 < NCH:
                c2_loads(ch0 + 2)
            c2_ptrans(ch0 + 1)
            c2_outproj(m, 1)
            c2_chainA(m, 1)
            T(1)
            c2_gate(m, 0)
            T(2)
            c2_chainA2(m, 1)
            T(2)
            c2_gate(m, 1)
            T(10)

    for l in range(L):
        xsrc = x_in if l == 0 else x1_d
        xdst = out_d if l == L - 1 else x1_d
        pass1(l, xsrc)
        pass2(l, xsrc, xdst)

    P.finalize(nc, stack)
    stack.close()
    return nc, P


def _host_inputs(S, inp, b):
    f = np.float32
    L = inp["w_in"].shape[0]
    bf = ml_dtypes.bfloat16
    k = np.arange(128)
    tri = (k[:, None] <= k[None, :]).astype(f)
    triu = (k[:, None] > k[None, :]).astype(f)
    negm = np.tile(np.where(k[None, :] < k[:, None], NEG, 0.0).astype(f), (1, 4)).astype(bf)
    m = {
        "x": np.ascontiguousarray(inp["x"][b, :S], dtype=f),
        "p": np.ascontiguousarray(inp["p"][:, b, :S], dtype=f),
        "w_in": np.ascontiguousarray(inp["w_in"], dtype=f),
        "w_out": np.ascontiguousarray(inp["w_out"], dtype=f),
        "w_gate": np.ascontiguousarray(inp["w_ple_gate"], dtype=f),
        "w_ple": np.ascontiguousarray(inp["w_ple_proj"], dtype=f),
        "npre": np.ascontiguousarray(inp["norm_pre"].reshape(L, 8, 128).transpose(0, 2, 1), dtype=f),
        "snorm": np.ascontiguousarray(inp["ssd_norm"].reshape(L, 8, 128).transpose(0, 2, 1), dtype=f),
        "npost": np.ascontiguousarray(inp["norm_post"], dtype=f),
        "cw": np.ascontiguousarray(inp["ssd_conv_w"].reshape(L, 4, 10, 128).transpose(0, 3, 2, 1).reshape(L, 128, 40), dtype=f),
        "cb": np.ascontiguousarray(inp["ssd_conv_b"].reshape(L, 10, 128).transpose(0, 2, 1), dtype=f),
        "scw": np.ascontiguousarray(inp["sc_conv_w"].reshape(L, 3, 8, 128).transpose(0, 3, 2, 1).reshape(L, 128, 24), dtype=f),
        "dsk": np.ascontiguousarray(np.repeat(inp["d_skip"], 64, axis=1).reshape(L, 8, 128).transpose(0, 2, 1), dtype=f),
        "dtb": np.ascontiguousarray(inp["dt_bias"], dtype=f),
        "alog": np.ascontiguousarray(inp["a_log"], dtype=f),
        "ident": np.eye(128, dtype=f).astype(bf),
        "tri": tri, "triu": triu.astype(bf), "negm": negm, "ones": np.ones((128, 128), f),
    }
    return m


_CACHE = {}


def run(inp, S, cores, trace=False):
    inp = {k: np.asarray(v) for k, v in inp.items()}
    L = inp["w_in"].shape[0]
    key = (S, L)
    if key not in _CACHE:
        _CACHE[key] = build_program(S, L)[0]
    nc = _CACHE[key]
    in_maps = [_host_inputs(S, inp, b) for b in range(cores)]
    res = run_bass_kernel_spmd(nc, in_maps, core_ids=list(range(cores)))
    return np.stack([np.asarray(r["out"], dtype=np.float32) for r in res.results], axis=0)


def kernel(**inputs):
    x = np.asarray(inputs["x"])
    B, S, _ = x.shape
    return run(inputs, S, B)
```

```python
import os
import sys
import numpy as np
import ml_dtypes
import concourse.bass as bass
import concourse.mybir as mybir
from concourse.bass_utils import run_bass_kernel_spmd

F32 = mybir.dt.float32
BF16 = mybir.dt.bfloat16
AF = mybir.ActivationFunctionType
ALU = mybir.AluOpType

D = 1024
NCOLS = 6416
C1 = 2320
C2 = 4096
EPS = 1e-6
MT = 512
NEG = -30000.0
CUR = {"banks": None, "line": None}
STRICT_SAME_ENGINE = True


class Prog:
    ENG = ("sp", "pe", "act", "dve", "pool")

    def __init__(self):
        self.ops = []
        self.nbanks = 8
        self.ps_next = 0
        self.held = set()

    def op(self, eng, fn, r=(), w=()):
        self.ops.append(dict(eng=eng, fn=fn, r=tuple(r), w=tuple(w), dma=None, line=sys._getframe(1).f_lineno))

    def dma(self, fn, r=(), w=(), sem=None):
        self.ops.append(dict(eng="sp", fn=fn, r=tuple(r), w=tuple(w), dma=sem, line=sys._getframe(1).f_lineno))

    def bank(self, hold=False):
        for _ in range(self.nbanks):
            b = self.ps_next
            self.ps_next = (self.ps_next + 1) % self.nbanks
            if b not in self.held:
                break
        else:
            raise RuntimeError("all PSUM banks held")
        if hold:
            self.held.add(b)
        self.ops.append(dict(eng=None, fence=b))
        return b

    def free(self, *banks):
        for b in banks:
            self.held.discard(b)

    def finalize(self, nc, stack):
        ops = self.ops
        lim = int(os.environ.get("KLIMIT", "0"))
        if lim:
            kept, n = [], 0
            for o in ops:
                if o["eng"] is not None:
                    n += 1
                    if n > lim:
                        break
                kept.append(o)
            ops = kept
            print("KLIMIT: last op line", ops[-1].get("line"), ops[-1]["eng"])
        last_w = {}
        readers = {}
        touched = {b: [] for b in range(self.nbanks)}
        bank_last = {b: {} for b in range(self.nbanks)}
        fence = {b: [] for b in range(self.nbanks)}
        real = []
        last_eng = {}
        last_dma = {}
        bar = set()
        for o in ops:
            if o["eng"] is None:
                if o.get("barrier"):
                    bar = set(last_eng.values()) | set(last_dma.values())
                    continue
                b = o["fence"]
                fence[b] = touched[b]
                touched[b] = []
                continue
            i = len(real)
            real.append(o)
            deps = set(bar)
            raw = set(bar)
            if o["dma"] is not None:
                last_dma[o["dma"]] = i
            else:
                last_eng[o["eng"]] = i
            for k in o["r"]:
                if k in last_w:
                    deps.add(last_w[k]); raw.add(last_w[k])
            for k in o["w"]:
                if k in last_w:
                    deps.add(last_w[k])
                for rr in readers.get(k, ()):
                    deps.add(rr)
            banks = set(k[1] for k in o["r"] + o["w"] if isinstance(k, tuple) and k[0] == "ps")
            for b in banks:
                for e2, j2 in bank_last[b].items():
                    if e2 != o["eng"]:
                        deps.add(j2)
                bank_last[b][o["eng"]] = i
            for k in o["r"]:
                readers.setdefault(k, []).append(i)
            for k in o["w"]:
                last_w[k] = i
                readers[k] = []
            deps.discard(i)
            o["deps"] = deps
            o["raw"] = raw
        need = [False] * len(real)
        for i, o in enumerate(real):
            best = {}
            for j in o["deps"]:
                pj = real[j]
                if pj["dma"] is not None:
                    kk = ("dma", pj["dma"])
                else:
                    if pj["eng"] == o["eng"]:
                        if o["eng"] == "pe":
                            continue
                        if j not in o["raw"] and not STRICT_SAME_ENGINE:
                            continue
                    kk = ("eng", pj["eng"])
                if j > best.get(kk, -1):
                    best[kk] = j
            keep = set(best.values())
            for j in keep:
                if real[j]["dma"] is None:
                    need[j] = True
            o["deps"] = keep
        cnt = {e: 0 for e in self.ENG}
        dcnt = {}
        for i, o in enumerate(real):
            if o["dma"] is not None:
                dcnt[o["dma"]] = dcnt.get(o["dma"], 0) + 16
                o["sig"] = (("dma", o["dma"]), dcnt[o["dma"]])
            elif need[i]:
                cnt[o["eng"]] += 1
                o["sig"] = (("eng", o["eng"]), cnt[o["eng"]])
            else:
                o["sig"] = None
        known = {e: {} for e in self.ENG}
        for o in real:
            want = {}
            for j in o["deps"]:
                s, v = real[j]["sig"]
                if v > want.get(s, 0):
                    want[s] = v
            kn = known[o["eng"]]
            waits = []
            for s, v in want.items():
                if kn.get(s, 0) >= v:
                    continue
                kn[s] = v
                waits.append((s, v))
            o["waits"] = waits
        sems = {}
        for e in self.ENG:
            if e != "sp":
                sems[("eng", e)] = stack.enter_context(nc.semaphore("s_" + e))
        for k in dcnt:
            sems[("dma", k)] = stack.enter_context(nc.semaphore("d_" + str(k)))
        self.stats = dict(n=len(real), sig=dict(cnt), dma=len(dcnt))
        final = [(("dma", k), v) for k, v in dcnt.items()]
        final += [(("eng", e), c) for e, c in cnt.items() if e != "sp" and c > 0]

        def emit(engname, e):
            for o in real:
                if o["eng"] != engname:
                    continue
                for s, v in o["waits"]:
                    e.wait_ge(sems[s], v)
                CUR["banks"] = set(k[1] for k in o["r"] + o["w"] if isinstance(k, tuple) and k[0] == "ps")
                CUR["line"] = o.get("line")
                ins = o["fn"](e)
                CUR["banks"] = None
                if o["sig"] is not None:
                    s, v = o["sig"]
                    ins.then_inc(sems[s], 16 if s[0] == "dma" else 1)
            if engname == "sp":
                for s, v in final:
                    e.wait_ge(sems[s], v)

        with nc.Block() as blk:
            blk.sync(lambda e: emit("sp", e))
            blk.tensor(lambda e: emit("pe", e))
            blk.scalar(lambda e: emit("act", e))
            blk.vector(lambda e: emit("dve", e))
            blk.gpsimd(lambda e: emit("pool", e))


def build_program(S, n_layers=2, debug=None):
    import contextlib
    assert S % MT == 0
    NM = S // MT
    NCH = S // 128
    nc = bass.Bass("TRN2", target_bir_lowering=False)
    stack = contextlib.ExitStack()
    stack.enter_context(nc.allow_low_precision(reason="bf16 matmul operands, fp32 accumulate"))
    P = Prog()

    def din(name, shape, dt=F32):
        return nc.dram_tensor(name, list(shape), dt, kind="ExternalInput").ap()

    L = n_layers
    x_in = din("x", [S, D])
    p_in = din("p", [L, S, 256])
    w_in = din("w_in", [L, D, NCOLS])
    w_out = din("w_out", [L, 2048, D])
    w_gate = din("w_gate", [L, D, D])
    w_ple = din("w_ple", [L, 256, D])
    npre_d = din("npre", [L, 128, 8])
    snorm_d = din("snorm", [L, 128, 8])
    npost_d = din("npost", [L, D])
    cw_d = din("cw", [L, 128, 40])
    cb_d = din("cb", [L, 128, 10])
    scw_d = din("scw", [L, 128, 24])
    dsk_d = din("dsk", [L, 128, 8])
    dtb_d = din("dtb", [L, 16])
    alog_d = din("alog", [L, 16])
    ident_d = din("ident", [128, 128], BF16)
    tri_d = din("tri", [128, 128])
    triu_d = din("triu", [128, 128], BF16)
    negm_d = din("negm", [128, 512], BF16)
    ones_d = din("ones", [128, 128])
    out_d = nc.dram_tensor("out", [S, D], F32, kind="ExternalOutput").ap()
    x1_d = nc.dram_tensor("x1_scr", [S, D], F32, kind="Internal").ap()
    yn_d = nc.dram_tensor("yn_scr", [NCH, 128, 1024], BF16, kind="Internal").ap()
    hT_d = nc.dram_tensor("hT_scr", [NM, 128, 8 * MT], BF16, kind="Internal").ap()
    dbg_outs = {}

    def sb(name, shape, dt=F32):
        return nc.alloc_sbuf_tensor("sb_" + name, list(shape), dt)

    arena = {}

    def arena_init():
        nbytes = (nc.sbuf_bytes_remaining - 64) // 64 * 64
        arena["t"] = nc.alloc_sbuf_tensor("arena", [128, nbytes // 2], BF16)
        arena["n"] = nbytes
        arena["off"] = 0

    def ab(name, shape, dt=F32):
        esz = 4 if dt == F32 else 2
        n = 1
        for s_ in shape[1:]:
            n *= s_
        nb = (n * esz + 31) // 32 * 32
        off = arena["off"]
        assert off + nb <= arena["n"], ("arena overflow", name, off, nb, arena["n"])
        arena["off"] = off + nb
        a = arena["t"][0:shape[0], off // 2:off // 2 + (n * esz) // 2]
        if dt == F32:
            a = a.bitcast(F32)
        if len(shape) == 3:
            a = a.rearrange("p (a b) -> p a b", a=shape[1])
        return a

    pbank = [nc.alloc_psum_tensor("pb%d" % i, [128, 512], F32) for i in range(8)]

    def pf(b):
        assert CUR["banks"] is None or b in CUR["banks"], ("undeclared PSUM bank access", b, CUR["line"])
        return pbank[b][:, :]

    def pbf(b):
        assert CUR["banks"] is None or b in CUR["banks"], ("undeclared PSUM bank access", b, CUR["line"])
        return pbank[b][:, :].bitcast(BF16)

    ident = sb("ident", [128, 128], BF16)
    tri = sb("tri", [128, 128])
    triub = sb("triub", [128, 128], BF16)
    negm = sb("negm", [128, 512], BF16)
    ones = sb("ones", [128, 128])
    neghalf = sb("neghalf", [128, 2])
    for t, d_, nm in ((ident, ident_d, "ident"), (tri, tri_d, "tri"), (triub, triu_d, "triub"),
                      (negm, negm_d, "negm"), (ones, ones_d, "ones")):
        P.dma(lambda e, t=t, d_=d_: e.dma_start(out=t[:, :], in_=d_[:, :]), w=[nm], sem="c_" + nm)
    P.op("pool", lambda e: e.memset(neghalf[:, :], -0.5), w=["neghalf"])

    xt = [sb("xt%d" % i, [128, D]) for i in range(2)]
    junk = sb("junk", [128, D], BF16)
    hb = [sb("hb%d" % i, [128, D], BF16) for i in range(2)]
    sm = [sb("sm%d" % i, [128, 8]) for i in range(4)]
    npre_t = sb("npre_t", [128, 8])
    arena_init()
    cur = {}

    def barrier():
        P.ops.append(dict(eng=None, barrier=True))

    cnt = dict(x=0, stg=0, wop=0, sm=0)

    def load_w(dst_ap, dst_key, src_ap, ncols, scale_ap=None, scale_key=None, cscale=None):
        stg = cur["stg"]
        i = cnt["stg"]; cnt["stg"] += 1
        s = i % len(stg)
        dst_key = (dst_key, i)
        P.dma(lambda e: e.dma_start(out=stg[s][:, 0:ncols], in_=src_ap), w=[("stg", s)], sem="stg%d" % s)
        eng = ("act", "dve", "pool")[cnt["wop"] % 3]; cnt["wop"] += 1
        rk = [("stg", s)] + ([scale_key] if scale_key else [])
        src = stg[s][:, 0:ncols]
        if scale_ap is None and cscale is None:
            if eng == "act":
                P.op("act", lambda e: e.activation(out=dst_ap, in_=src, func=AF.Copy), r=rk, w=[dst_key])
            else:
                P.op(eng, lambda e: e.tensor_copy(out=dst_ap, in_=src), r=rk, w=[dst_key])
        elif scale_ap is None:
            if eng == "act":
                P.op("act", lambda e: e.activation(out=dst_ap, in_=src, func=AF.Copy, scale=float(cscale)), r=rk, w=[dst_key])
            else:
                P.op(eng, lambda e: e.tensor_scalar(out=dst_ap, in0=src, scalar1=float(cscale), scalar2=0.0,
                                                    op0=ALU.mult, op1=ALU.add), r=rk, w=[dst_key])
        else:
            if eng == "act":
                P.op("act", lambda e: e.activation(out=dst_ap, in_=src, func=AF.Copy, scale=scale_ap), r=rk, w=[dst_key])
            else:
                P.op(eng, lambda e: e.tensor_scalar(out=dst_ap, in0=src, scalar1=scale_ap, scalar2=0.0,
                                                    op0=ALU.mult, op1=ALU.add), r=rk, w=[dst_key])

    def small_dma(dst_t, dst_key, src_ap, sem):
        P.dma(lambda e: e.dma_start(out=dst_t, in_=src_ap), w=[dst_key], sem=sem)

    def phaseA(xsrc, m):
        for c in range(cur["MT"] // 128):
            phaseA_chunk(xsrc, m, c)

    def phaseA_chunk(xsrc, m, c):
        phaseA_ew(xsrc, m, c)
        phaseA_pe(m, c)

    pa_state = {}

    def phaseA_ew(xsrc, m, c):
        t0 = m * cur["MT"] + c * 128
        i = cnt["x"]; cnt["x"] += 1
        s = i % 2
        q = cnt["sm"] % 4; cnt["sm"] += 1
        pa_state[(m, c)] = s
        P.dma(lambda e, s=s, t0=t0: e.dma_start(out=xt[s][:, :], in_=xsrc[t0:t0 + 128, :]),
              w=[("xt", s)], sem="xt%d" % s)
        P.op("act", lambda e, s=s, q=q: e.activation(out=junk[:, :], in_=xt[s][:, :], func=AF.Square,
                                                      accum_out=sm[q][:, 0:1]),
             r=[("xt", s)], w=["junk", ("sm", q, 0)])
        P.op("pool", lambda e, q=q: e.tensor_scalar(out=sm[q][:, 1:2], in0=sm[q][:, 0:1], scalar1=1.0 / D,
                                                     scalar2=EPS, op0=ALU.mult, op1=ALU.add),
             r=[("sm", q, 0)], w=[("sm", q, 1)])
        P.op("pool", lambda e, q=q: e.tensor_tensor(out=sm[q][:, 2:3], in0=sm[q][:, 1:2], in1=neghalf[:, 0:1],
                                                     op=ALU.pow),
             r=[("sm", q, 1), "neghalf"], w=[("sm", q, 2)])
        P.op("dve", lambda e, s=s, q=q: e.tensor_scalar(out=hb[s][:, :], in0=xt[s][:, :], scalar1=sm[q][:, 2:3],
                                                         scalar2=None, op0=ALU.mult),
             r=[("xt", s), ("sm", q, 2)], w=[("hb", s)])

    def phaseA_pe(m, c):
        hT = cur["hT"]
        par = m % len(hT)
        s = pa_state.pop((m, c))
        b = P.bank()
        for kc in range(8):
            P.op("pe", lambda e, s=s, kc=kc, b=b: e.transpose(out=pbf(b)[:, kc * 128:(kc + 1) * 128],
                                                              in_=hb[s][:, kc * 128:(kc + 1) * 128],
                                                              identity=ident[:, :]),
                 r=[("hb", s), "ident"], w=[("ps", b)])
        P.op("act", lambda e, b=b, par=par, c=c: e.activation(
            out=hT[par][:, :, c * 128:(c + 1) * 128],
            in_=pbf(b).rearrange("p (k t) -> p k t", k=8), func=AF.Copy),
            r=[("ps", b)], w=[("hT", par, c)])

    hT_keys = lambda par: [("hT", par, c) for c in range(cur["MT"] // 128)]

    def pass1(l, xsrc):
        barrier()
        arena["off"] = 0
        def load_consts_p1():
            small_dma(npre_t[:, :], "npre_t", npre_d[l], "c_npre")
            small_dma(cw_t[:, :], "cw_t", cw_d[l], "c_cw")
            small_dma(cb_t[:, :], "cb_t", cb_d[l], "c_cb")
            small_dma(dsk_t[:, :], "dsk_t", dsk_d[l], "c_dsk")
            small_dma(dtb_b[:, :], "dtb_b", dtb_d[l:l + 1, :].partition_broadcast(128), "c_dtb")
            small_dma(a_b[:, :], "a_b", alog_d[l:l + 1, :].partition_broadcast(128), "c_alog")
            P.op("act", lambda e: e.activation(out=a_b[:, :], in_=a_b[:, :], func=AF.Exp), r=["a_b"], w=["a_b"])
            P.op("dve", lambda e: e.tensor_scalar(out=a_b[:, :], in0=a_b[:, :], scalar1=-1.0, scalar2=None, op0=ALU.mult),
                 r=["a_b"], w=["a_b"])
            P.op("dve", lambda e: e.tensor_tensor(
                out=dgx[:, :, :], in0=ident[:, :].unsqueeze(1).broadcast_to([128, 40, 128]),
                in1=cw_t[:, :].unsqueeze(2).broadcast_to([128, 40, 128]), op=ALU.mult),
                r=["ident", "cw_t"], w=["dgx"])
            P.op("dve", lambda e: e.tensor_tensor(
                out=dskd[:, :, :], in0=ident[:, :].unsqueeze(1).broadcast_to([128, 8, 128]),
                in1=dsk_t[:, :].unsqueeze(2).broadcast_to([128, 8, 128]), op=ALU.mult),
                r=["ident", "dsk_t"], w=["dskd"])
            for kc in range(8):
                for c0, c1 in ((0, 1024), (1024, 2048), (2048, C1)):
                    load_w(w1[:, kc, c0:c1], "w1", w_in[l, kc * 128:(kc + 1) * 128, c0:c1], c1 - c0,
                           scale_ap=npre_t[:, kc:kc + 1], scale_key="npre_t")

        w1 = ab("w1", [128, 8, C1], BF16)
        dgx = ab("dgx", [128, 40, 128], BF16)
        dskd = ab("dskd", [128, 8, 128], BF16)
        cw_t = ab("cw_t", [128, 40])
        cb_t = ab("cb_t", [128, 10])
        dsk_t = ab("dsk_t", [128, 8])
        dtb_b = ab("dtb_b", [128, 16])
        a_b = ab("a_b", [128, 16])
        off_work = arena["off"]
        stg = [ab("stg%d" % i, [128, 1024]) for i in range(16)]
        cur["stg"] = stg
        load_consts_p1()
        barrier()
        arena["off"] = off_work
        hT = [ab("hT%d" % i, [128, 8, MT], BF16) for i in range(2)]
        ub = ab("ub", [128, 10, MT + 4], BF16)
        xbcT = [ab("xbcT%d" % i, [128, 10, MT], BF16) for i in range(2)]
        th_sb = ab("th_sb", [128, D], BF16)
        g_sb = [ab("g_sb%d" % i, [128, D], BF16) for i in range(2)]
        D4 = [ab("dts4_%d" % i, [128, 8, 64]) for i in range(2)]
        A2 = [ab("adt24_%d" % i, [128, 2, 64], BF16) for i in range(2)]
        E4 = [ab("etot4_%d" % i, [128, 4, 8]) for i in range(2)]
        Xs = [ab("Xs%d" % i, [128, 32, 128], BF16) for i in range(2)]
        Eb = [ab("Eb%d" % i, [128, 16, 128], BF16) for i in range(2)]
        Mt = [ab("Mt%d" % i, [128, 16, 128], BF16) for i in range(2)]
        btok = [ab("btok%d" % i, [128, 128], BF16) for i in range(2)]
        cbs = [ab("cbs%d" % i, [128, 256], BF16) for i in range(2)]
        xd = [ab("xd%d" % i, [128, D], BF16) for i in range(2)]
        xdd = [ab("xdd%d" % i, [128, D], BF16) for i in range(2)]
        Sst = ab("Sst", [128, 512])
        Sbf = ab("Sbf", [128, 512], BF16)
        ty = [ab("ty%d" % i, [128, D]) for i in range(2)]
        yz = [ab("yz%d" % i, [128, D]) for i in range(2)]
        ssg = [ab("ssg%d" % i, [128, 8]) for i in range(2)]
        yn = [ab("yn%d" % i, [128, D], BF16) for i in range(2)]
        ynT = [ab("ynT%d" % i, [128, 8, 128], BF16) for i in range(2)]
        cur["hT"] = hT
        cur["stg"] = stg
        cur["MT"] = MT

        P.op("pool", lambda e: e.memset(ub[:, :, :], 0.0), w=[("ub", j) for j in range(10)])
        P.op("pool", lambda e: e.memset(Sst[:, :], 0.0), w=["Sst"])
        P.op("pool", lambda e: e.memset(Sbf[:, :], 0.0), w=["Sbf"])

        def b1_halo():
            P.op("dve", lambda e: e.tensor_copy(out=ub[:, :, 0:3], in_=ub[:, :, MT:MT + 3]),
                 r=[("ub", j) for j in range(10)], w=[("ub", j) for j in range(10)])

        b1_bank = {}

        def b1_proj(m, j):
            par = m % 2
            b = P.bank(hold=True)
            b1_bank[(m, j)] = b
            for kc in range(8):
                P.op("pe", lambda e, kc=kc: e.matmul(
                    pf(b), lhsT=w1[:, kc, j * 128:(j + 1) * 128], rhs=hT[par][:, kc, :],
                    start=(kc == 0), stop=(kc == 7)),
                    r=["w1"] + hT_keys(par), w=[("ps", b)])
            P.op("act", lambda e: e.activation(out=ub[:, j, 3:MT + 3], in_=pf(b), func=AF.Copy),
                 r=[("ps", b)], w=[("ub", j)])
            P.free(b)

        def b1_conv(m, j):
            par = m % 2
            b2 = P.bank()
            for k in range(4):
                P.op("pe", lambda e, k=k: e.matmul(
                    pf(b2), lhsT=dgx[:, j * 4 + k, :], rhs=ub[:, j, k:k + MT], start=(k == 0), stop=(k == 3)),
                    r=["dgx", ("ub", j)], w=[("ps", b2)])
            P.op("act", lambda e: e.activation(
                out=xbcT[par][:, j, :], in_=pf(b2), func=AF.Silu, bias=cb_t[:, j:j + 1]),
                r=[("ps", b2), "cb_t"], w=[("xbcT", par, j)])

        def b1_step(m, k):
            if k == 0:
                b1_halo()
            if k < 10:
                b1_proj(m, k)
            if 2 <= k < 12:
                b1_conv(m, k - 2)

        def ctx(m, c):
            par = m % 2
            ch = m * 4 + c
            s = ch % 2
            return par, ch, s, slice(c * 128, (c + 1) * 128), [("xbcT", par, j) for j in range(10)]

        def s0(m):
            par = m % 2
            d = D4[par]
            dk = lambda i: ("dts4", par, i)
            bs = P.bank()
            for c in range(4):
                ts = slice(c * 128, (c + 1) * 128)
                for kc in range(8):
                    P.op("pe", lambda e, kc=kc, c=c, ts=ts: e.matmul(
                        pf(bs)[:, c * 16:(c + 1) * 16], lhsT=hT[par][:, kc, ts], rhs=w1[:, kc, 1280:1296],
                        start=(kc == 0), stop=(kc == 7)),
                        r=["w1", ("hT", par, c)], w=[("ps", bs)])
            P.op("dve", lambda e: e.tensor_tensor(
                out=d[:, 0, :].rearrange("p (c h) -> p c h", c=4), in0=pf(bs)[:, 0:64].rearrange("p (c h) -> p c h", c=4),
                in1=dtb_b[:, :].unsqueeze(1).broadcast_to([128, 4, 16]), op=ALU.add),
                r=[("ps", bs), "dtb_b"], w=[dk(0)])
            P.op("dve", lambda e: e.tensor_scalar(out=d[:, 1, :], in0=d[:, 0, :], scalar1=-1.0, scalar2=None, op0=ALU.mult),
                 r=[dk(0)], w=[dk(1)])
            P.op("dve", lambda e: e.tensor_tensor(out=d[:, 1, :], in0=d[:, 1, :], in1=d[:, 0, :], op=ALU.min),
                 r=[dk(0), dk(1)], w=[dk(1)])
            P.op("act", lambda e: e.activation(out=d[:, 1, :], in_=d[:, 1, :], func=AF.Exp), r=[dk(1)], w=[dk(1)])
            P.op("act", lambda e: e.activation(out=d[:, 2, :], in_=d[:, 1, :], func=AF.Ln, bias=1.0), r=[dk(1)], w=[dk(2)])
            P.op("dve", lambda e: e.scalar_tensor_tensor(out=d[:, 3, :], in0=d[:, 0, :], scalar=0.0, in1=d[:, 2, :],
                                                         op0=ALU.max, op1=ALU.add), r=[dk(0), dk(2)], w=[dk(3)])
            P.op("dve", lambda e: e.tensor_tensor(
                out=d[:, 4, :].rearrange("p (c h) -> p c h", c=4), in0=d[:, 3, :].rearrange("p (c h) -> p c h", c=4),
                in1=a_b[:, :].unsqueeze(1).broadcast_to([128, 4, 16]), op=ALU.mult),
                r=[dk(3), "a_b"], w=[dk(4)])
            P.op("dve", lambda e: e.tensor_copy(out=A2[par][:, 0, :], in_=d[:, 4, :]), r=[dk(4)], w=[("adt2", par, 0)])
            P.op("dve", lambda e: e.tensor_tensor(out=A2[par][:, 1, :], in0=d[:, 4, :], in1=A2[par][:, 0, :], op=ALU.subtract),
                 r=[dk(4), ("adt2", par, 0)], w=[("adt2", par, 1)])

        def s0b(m):
            par = m % 2
            d = D4[par]
            dk = lambda i: ("dts4", par, i)
            b2 = P.bank()
            P.op("pe", lambda e: e.matmul(pf(b2)[:, 0:64], lhsT=tri[:, :], rhs=d[:, 4, :], start=True, stop=True),
                 r=["tri", dk(4)], w=[("ps", b2, "cs")])
            P.op("pe", lambda e: e.matmul(pf(b2)[:, 64:128], lhsT=ones[:, :], rhs=d[:, 4, :], start=True, stop=True),
                 r=["ones", dk(4)], w=[("ps", b2, "tot")])
            P.op("act", lambda e: e.activation(out=d[:, 5, :], in_=pf(b2)[:, 0:64], func=AF.Exp),
                 r=[("ps", b2, "cs")], w=[dk(5)])
            P.op("dve", lambda e: e.tensor_copy(out=d[:, 6, :], in_=pf(b2)[:, 0:64]), r=[("ps", b2, "cs")], w=[dk(6)])
            P.op("dve", lambda e: e.tensor_tensor(out=d[:, 6, :], in0=pf(b2)[:, 64:128], in1=d[:, 6, :], op=ALU.subtract),
                 r=[("ps", b2, "tot"), dk(6)], w=[dk(6)])
            P.op("act", lambda e: e.activation(out=d[:, 7, :], in_=d[:, 6, :], func=AF.Exp), r=[dk(6)], w=[dk(7)])
            P.op("act", lambda e: e.activation(
                out=E4[par][0:64, :, :], in_=pf(b2)[0:64, 64:128].rearrange("p (c h) -> p c h", c=4)[:, :, 0:8], func=AF.Exp),
                r=[("ps", b2, "tot")], w=[("etot", par, 0)])
            P.op("act", lambda e: e.activation(
                out=E4[par][64:128, :, :], in_=pf(b2)[64:128, 64:128].rearrange("p (c h) -> p c h", c=4)[:, :, 8:16], func=AF.Exp),
                r=[("ps", b2, "tot")], w=[("etot", par, 1)])

        def s1(m, c):
            par, ch, s, ts, xk = ctx(m, c)
            bz = [P.bank(), P.bank()]
            for nb in range(2):
                for kc in range(8):
                    P.op("pe", lambda e, nb=nb, kc=kc: e.matmul(
                        pf(bz[nb]), lhsT=hT[par][:, kc, ts], rhs=w1[:, kc, 1296 + nb * 512:1296 + (nb + 1) * 512],
                        start=(kc == 0), stop=(kc == 7)),
                        r=["w1", ("hT", par, c)], w=[("ps", bz[nb])])
            for nb in range(2):
                sl = slice(nb * 512, (nb + 1) * 512)
                P.op("act", lambda e, nb=nb, sl=sl: e.activation(out=g_sb[s][:, sl], in_=pf(bz[nb]), func=AF.Silu),
                     r=[("ps", bz[nb])], w=[("g", s, nb)])

        def sX(m, c):
            par, ch, s, ts, xk = ctx(m, c)
            P.op("pool", lambda e: e.tensor_tensor(
                out=Xs[s][:, 0:16, :], in0=tri[:, :].unsqueeze(1).broadcast_to([128, 16, 128]),
                in1=A2[par][:, 0, c * 16:(c + 1) * 16].unsqueeze(2).broadcast_to([128, 16, 128]), op=ALU.mult),
                r=["tri", ("adt2", par, 0)], w=[("Xs", s, 0)])
            P.op("dve", lambda e: e.tensor_tensor(
                out=Xs[s][:, 16:32, :], in0=tri[:, :].unsqueeze(1).broadcast_to([128, 16, 128]),
                in1=A2[par][:, 1, c * 16:(c + 1) * 16].unsqueeze(2).broadcast_to([128, 16, 128]), op=ALU.mult),
                r=["tri", ("adt2", par, 1)], w=[("Xs", s, 1)])

        def s2a(m, c):
            par, ch, s, ts, xk = ctx(m, c)
            d = D4[par][:, :, c * 16:(c + 1) * 16]
            dk = lambda i: ("dts4", par, i)
            bt = P.bank()
            for j in range(8):
                P.op("pe", lambda e, j=j: e.transpose(out=pbf(bt)[:, j * 128:(j + 1) * 128], in_=xbcT[par][:, j, ts],
                                                      identity=ident[:, :]),
                     r=[xk[j], "ident"], w=[("ps", bt)])
            bb = P.bank()
            P.op("pe", lambda e: e.transpose(out=pbf(bb)[:, 0:128], in_=xbcT[par][:, 8, ts], identity=ident[:, :]),
                 r=[xk[8], "ident"], w=[("ps", bb)])
            bc = [P.bank(), P.bank()]
            for g in range(2):
                gs = slice(g * 64, (g + 1) * 64)
                P.op("pe", lambda e, g=g, gs=gs: e.matmul(
                    pf(bc[g])[:, 0:128], lhsT=xbcT[par][gs, 8, ts], rhs=xbcT[par][gs, 9, ts],
                    start=True, stop=True), r=[xk[8], xk[9]], w=[("ps", bc[g])])
            P.op("dve", lambda e: e.tensor_tensor(
                out=xd[s][:, :].rearrange("p (h q) -> p h q", h=16), in0=pbf(bt).rearrange("p (h q) -> p h q", h=16),
                in1=d[:, 3, :].unsqueeze(2).broadcast_to([128, 16, 64]), op=ALU.mult),
                r=[("ps", bt), dk(3)], w=[("xd", s)])
            P.op("act", lambda e: e.activation(out=btok[s][:, :], in_=pbf(bb)[:, 0:128], func=AF.Copy),
                 r=[("ps", bb)], w=[("btok", s)])
            for g in range(2):
                P.op("dve", lambda e, g=g: e.tensor_copy(out=cbs[s][:, g * 128:(g + 1) * 128], in_=pf(bc[g])[:, 0:128]),
                     r=[("ps", bc[g])], w=[("cbs", s, g)])

        def s2b(m, c):
            par, ch, s, ts, xk = ctx(m, c)
            d = D4[par][:, :, c * 16:(c + 1) * 16]
            dk = lambda i: ("dts4", par, i)
            for q in range(4):
                bq = P.bank()
                P.op("pe", lambda e, bq=bq, q=q: e.matmul(
                    pf(bq), lhsT=triub[:, :], rhs=Xs[s][:, 4 * q:4 * q + 4, :].rearrange("p h l -> p (h l)"),
                    start=True, stop=False), r=["triub", ("Xs", s, 0)], w=[("ps", bq)])
                P.op("pe", lambda e, bq=bq, q=q: e.matmul(
                    pf(bq), lhsT=triub[:, :], rhs=Xs[s][:, 16 + 4 * q:16 + 4 * q + 4, :].rearrange("p h l -> p (h l)"),
                    start=False, stop=False), r=["triub", ("Xs", s, 1)], w=[("ps", bq)])
                P.op("pe", lambda e, bq=bq: e.matmul(pf(bq), lhsT=ident[:, :], rhs=negm[:, :], start=False, stop=True),
                     r=["ident", "negm"], w=[("ps", bq)])
                P.op("act", lambda e, bq=bq, q=q: e.activation(
                    out=Eb[s][:, 4 * q:4 * q + 4, :].rearrange("p h l -> p (h l)"), in_=pf(bq), func=AF.Exp),
                    r=[("ps", bq)], w=[("Eb", s, q)])

        def s2c(m, c):
            par, ch, s, ts, xk = ctx(m, c)
            d = D4[par][:, :, c * 16:(c + 1) * 16]
            dk = lambda i: ("dts4", par, i)
            P.op("pool", lambda e: e.tensor_tensor(
                out=xdd[s][:, :].rearrange("p (h q) -> p h q", h=16), in0=xd[s][:, :].rearrange("p (h q) -> p h q", h=16),
                in1=d[:, 7, :].unsqueeze(2).broadcast_to([128, 16, 64]), op=ALU.mult),
                r=[("xd", s), dk(7)], w=[("xdd", s)])
            P.op("dve", lambda e: e.tensor_tensor(
                out=Mt[s][:, :, :].rearrange("p (g r) l -> p g r l", g=2),
                in0=Eb[s][:, :, :].rearrange("p (g r) l -> p g r l", g=2),
                in1=cbs[s][:, :].rearrange("p (g l) -> p g l", g=2).unsqueeze(2).broadcast_to([128, 2, 8, 128]),
                op=ALU.mult), r=[("Eb", s, q) for q in range(4)] + [("cbs", s, 0), ("cbs", s, 1)], w=[("Mt", s)])

        def s3(m, c):
            par, ch, s, ts, xk = ctx(m, c)
            d = D4[par][:, :, c * 16:(c + 1) * 16]
            dk = lambda i: ("dts4", par, i)
            bn = P.bank()
            for g in range(2):
                gs = slice(g * 64, (g + 1) * 64)
                P.op("pe", lambda e, g=g, gs=gs: e.matmul(
                    pf(bn)[gs, :], lhsT=btok[s][:, gs], rhs=xdd[s][:, g * 512:(g + 1) * 512], start=True, stop=True),
                    r=[("btok", s), ("xdd", s)], w=[("ps", bn, g)])
            P.op("dve", lambda e: e.tensor_tensor(
                out=Sst[:, :].rearrange("p (r q) -> p r q", r=8), in0=Sst[:, :].rearrange("p (r q) -> p r q", r=8),
                in1=E4[par][:, c, :].unsqueeze(2).broadcast_to([128, 8, 64]), op=ALU.mult),
                r=["Sst", ("etot", par, 0), ("etot", par, 1)], w=["Sst"])
            P.op("dve", lambda e: e.tensor_tensor(out=Sst[:, :], in0=pf(bn), in1=Sst[:, :], op=ALU.add),
                 r=["Sst", ("ps", bn, 0), ("ps", bn, 1)], w=["Sst"])
            by = [P.bank(), P.bank()]
            for nb in range(2):
                for jj in range(4):
                    j = nb * 4 + jj
                    P.op("pe", lambda e, nb=nb, jj=jj, j=j: e.matmul(
                        pf(by[nb])[:, jj * 128:(jj + 1) * 128], lhsT=xbcT[par][:, j, ts], rhs=dskd[:, j, :],
                        start=(jj == 0), stop=False, skip_group_check=True), r=[xk[j], "dskd"], w=[("ps", by[nb])])
                for hh in range(8):
                    h = nb * 8 + hh
                    P.op("pe", lambda e, nb=nb, hh=hh, h=h: e.matmul(
                        pf(by[nb])[:, hh * 64:(hh + 1) * 64], lhsT=Mt[s][:, h, :], rhs=xd[s][:, h * 64:(h + 1) * 64],
                        start=False, stop=(hh == 7), skip_group_check=True), r=[("Mt", s), ("xd", s)], w=[("ps", by[nb])])
            bo = [P.bank(), P.bank()]
            for g in range(2):
                gs = slice(g * 64, (g + 1) * 64)
                P.op("pe", lambda e, g=g, gs=gs: e.matmul(
                    pf(bo[g]), lhsT=xbcT[par][gs, 9, ts], rhs=Sbf[gs, :], start=True, stop=True),
                    r=[xk[9], "Sbf"], w=[("ps", bo[g])])
            P.op("act", lambda e: e.activation(out=Sbf[:, :], in_=Sst[:, :], func=AF.Copy), r=["Sst"], w=["Sbf"])
            for g in range(2):
                sl = slice(g * 512, (g + 1) * 512)
                P.op("dve", lambda e, g=g, sl=sl: e.tensor_tensor(
                    out=ty[s][:, sl].rearrange("p (r q) -> p r q", r=8), in0=pf(bo[g]).rearrange("p (r q) -> p r q", r=8),
                    in1=d[:, 5, g * 8:(g + 1) * 8].unsqueeze(2).broadcast_to([128, 8, 64]), op=ALU.mult),
                    r=[("ps", bo[g]), dk(5)], w=[("ty", s, g)])
                P.op("dve", lambda e, g=g, sl=sl: e.tensor_tensor(out=ty[s][:, sl], in0=pf(by[g]), in1=ty[s][:, sl], op=ALU.add),
                     r=[("ps", by[g]), ("ty", s, g)], w=[("ty", s, g)])
                P.op("pool", lambda e, g=g, sl=sl: e.tensor_tensor(out=yz[s][:, sl], in0=ty[s][:, sl], in1=g_sb[s][:, sl], op=ALU.mult),
                     r=[("ty", s, g), ("g", s, g)], w=[("yz", s, g)])
                P.op("act", lambda e, g=g, sl=sl: e.activation(out=junk[:, sl], in_=yz[s][:, sl], func=AF.Square,
                                                              accum_out=ssg[s][:, g:g + 1]),
                     r=[("yz", s, g)], w=["junk", ("ssg", s, g)])

        def s3b(m, c):
            par, ch, s, ts, xk = ctx(m, c)
            q = ssg[s]
            P.op("pool", lambda e: e.tensor_scalar(out=q[:, 2:4], in0=q[:, 0:2], scalar1=1.0 / 512.0,
                                                   scalar2=EPS, op0=ALU.mult, op1=ALU.add),
                 r=[("ssg", s, 0), ("ssg", s, 1)], w=[("ssg", s, 2)])
            P.op("pool", lambda e: e.tensor_tensor(out=q[:, 6:8], in0=q[:, 2:4], in1=neghalf[:, 0:2], op=ALU.pow),
                 r=[("ssg", s, 2), "neghalf"], w=[("ssg", s, 6)])
            for g in range(2):
                sl = slice(g * 512, (g + 1) * 512)
                P.op("act", lambda e, g=g, sl=sl: e.activation(out=yn[s][:, sl], in_=yz[s][:, sl], func=AF.Copy,
                                                              scale=q[:, 6 + g:7 + g]),
                     r=[("yz", s, g), ("ssg", s, 6)], w=[("yn", s, g)])

        def s4(m, c):
            par, ch, s, ts, xk = ctx(m, c)
            bt2 = P.bank()
            for j in range(8):
                P.op("pe", lambda e, j=j: e.transpose(out=pbf(bt2)[:, j * 128:(j + 1) * 128],
                                                      in_=yn[s][:, j * 128:(j + 1) * 128], identity=ident[:, :]),
                     r=[("yn", s, j // 4), "ident"], w=[("ps", bt2)])
            P.op("dve", lambda e: e.tensor_copy(out=ynT[s][:, :, :].rearrange("p k t -> p (k t)"), in_=pbf(bt2)),
                 r=[("ps", bt2)], w=[("ynT", s)])
            P.dma(lambda e: e.dma_start(out=yn_d[ch], in_=ynT[s][:, :, :].rearrange("p k t -> p (k t)")),
                  r=[("ynT", s)], w=[("yn_d", ch)], sem="yno%d" % s)

        def hT_store(mm):
            par = mm % 2
            P.dma(lambda e: e.dma_start(out=hT_d[mm], in_=hT[par][:, :, :].rearrange("p k t -> p (k t)")),
                  r=[("hT", par, c) for c in range(4)], w=[("hT_d", mm)], sem="hTo%d" % par)

        phaseA(xsrc, 0)
        hT_store(0)
        s0(0)
        for k in range(12):
            b1_step(0, k)
        s0b(0)
        if NM > 1:
            phaseA(xsrc, 1)
            hT_store(1)
        pend4 = []
        for m in range(NM):
            nxt = m + 1 < NM
            tq = list(range(12)) if nxt else []
            if nxt:
                s0(m + 1)

            def T(n):
                for _ in range(n):
                    if tq:
                        b1_step(m + 1, tq.pop(0))
            for pi, pr in enumerate(((0, 1), (2, 3))):
                pa = m + 2 < NM
                for c in pr:
                    s1(m, c)
                    sX(m, c)
                while pend4:
                    s4(*pend4.pop(0))
                if pa:
                    for c in pr:
                        phaseA_ew(xsrc, m + 2, c)
                T(1)
                for c in pr:
                    s2a(m, c)
                if nxt and pi == 0:
                    s0b(m + 1)
                T(1)
                for c in pr:
                    s2b(m, c)
                for c in pr:
                    s2c(m, c)
                T(1)
                for c in pr:
                    s3(m, c)
                T(1)
                for c in pr:
                    s3b(m, c)
                if pa:
                    for c in pr:
                        phaseA_pe(m + 2, c)
                    if pi == 1:
                        hT_store(m + 2)
                for c in pr:
                    pend4.append((m, c))
                T(2 if pi == 0 else 12)

        while pend4:
            s4(*pend4.pop(0))

    def pass2(l, xsrc, xdst):
        barrier()
        arena["off"] = 0
        MT2 = 256
        CPM = MT2 // 128
        NM2 = S // MT2
        cur["MT"] = MT2
        w2 = ab("w2", [128, 8, C2], BF16)
        wo = ab("wo", [128, 16, D], BF16)
        wg = ab("wg", [128, 8, D], BF16)
        wp = ab("wp", [128, 2, D], BF16)
        dgs = ab("dgs", [128, 24, 128], BF16)
        scw_t = ab("scw_t", [128, 24])
        sn_t = ab("sn_t", [128, 8])
        npost_b = ab("npost_b", [128, D])
        off_work = arena["off"]
        stg = [ab("stg%d" % i, [128, 1024]) for i in range(14)]
        cur["stg"] = stg

        small_dma(npre_t[:, :], "npre_t", npre_d[l], "c_npre")
        small_dma(scw_t[:, :], "scw_t", scw_d[l], "c_scw")
        small_dma(sn_t[:, :], "sn_t", snorm_d[l], "c_sn")
        small_dma(npost_b[:, :], "npost_b", npost_d[l:l + 1, :].partition_broadcast(128), "c_npost")
        P.op("dve", lambda e: e.tensor_tensor(
            out=dgs[:, :, :], in0=ident[:, :].unsqueeze(1).broadcast_to([128, 24, 128]),
            in1=scw_t[:, :].unsqueeze(2).broadcast_to([128, 24, 128]), op=ALU.mult),
            r=["ident", "scw_t"], w=["dgs"])
        for kc in range(8):
            for q in range(4):
                load_w(w2[:, kc, q * 1024:(q + 1) * 1024], "w2",
                       w_in[l, kc * 128:(kc + 1) * 128, C1 + q * 1024:C1 + (q + 1) * 1024], 1024,
                       scale_ap=npre_t[:, kc:kc + 1], scale_key="npre_t")
        for kc in range(16):
            if kc < 8:
                load_w(wo[:, kc, :], "wo", w_out[l, kc * 128:(kc + 1) * 128, :], 1024,
                       scale_ap=sn_t[:, kc:kc + 1], scale_key="sn_t")
            else:
                load_w(wo[:, kc, :], "wo", w_out[l, kc * 128:(kc + 1) * 128, :], 1024)
        for kc in range(8):
            load_w(wg[:, kc, :], "wg", w_gate[l, kc * 128:(kc + 1) * 128, :], 1024)
        for kc in range(2):
            load_w(wp[:, kc, :], "wp", w_ple[l, kc * 128:(kc + 1) * 128, :], 1024, cscale=0.5)
        barrier()
        arena["off"] = off_work
        hT = [ab("hT%d" % i, [128, 8, MT2], BF16) for i in range(2)]
        cur["hT"] = hT
        x1bs = [ab("x1b%d" % i, [128, D], BF16) for i in range(2)]
        chb = ab("chb", [128, 8, MT2 + 4], BF16)
        hs = ab("hs", [128, MT2])
        zs = ab("zs", [128, MT2])
        tb = [ab("tb%d" % i, [128, MT2]) for i in range(3)]
        yscT = [ab("yscT%d" % i, [128, 8, MT2], BF16) for i in range(2)]
        ynTi = [ab("ynTi%d" % i, [128, 8, 128], BF16) for i in range(2)]
        xr = [ab("xr%d" % i, [128, D]) for i in range(2)]
        pt = [ab("pt%d" % i, [128, 256]) for i in range(2)]
        x1s = [ab("x1s%d" % i, [128, D]) for i in range(2)]
        x1T = [ab("x1T%d" % i, [128, 8, 128], BF16) for i in range(2)]
        pbb = ab("pbb", [128, 256], BF16)
        pT = [ab("pT%d" % i, [128, 2, 128], BF16) for i in range(2)]
        th2 = ab("th2", [128, D])
        ss2 = [ab("ss2_%d" % i, [128, 8]) for i in range(2)]
        P.op("pool", lambda e: e.memset(chb[:, :, :], 0.0), w=[("chb", j) for j in range(8)])
        print("pass2 arena used", arena["off"], "of", arena["n"])

        def c2_loads(ch):
            s = ch % 2
            t0 = ch * 128
            P.dma(lambda e: e.dma_start(out=ynTi[s][:, :, :].rearrange("p k t -> p (k t)"), in_=yn_d[ch]),
                  r=[("yn_d", ch)], w=[("ynTi", s)], sem="yni%d" % s)
            P.dma(lambda e: e.dma_start(out=xr[s][:, :], in_=xsrc[t0:t0 + 128, :]), w=[("xr", s)], sem="xr%d" % s)
            P.dma(lambda e: e.dma_start(out=pt[s][:, :], in_=p_in[l, t0:t0 + 128, :]), w=[("pt", s)], sem="pt%d" % s)

        st2 = {}

        def c2_ptrans(ch):
            s = ch % 2
            P.op("act", lambda e: e.activation(out=pbb[:, :], in_=pt[s][:, :], func=AF.Copy), r=[("pt", s)], w=["pbb"])
            bp = P.bank()
            for kc in range(2):
                P.op("pe", lambda e, kc=kc: e.transpose(out=pbf(bp)[:, kc * 128:(kc + 1) * 128],
                                                        in_=pbb[:, kc * 128:(kc + 1) * 128], identity=ident[:, :]),
                     r=["pbb", "ident"], w=[("ps", bp)])
            P.op("dve", lambda e: e.tensor_copy(out=pT[s][:, :, :].rearrange("p k t -> p (k t)"), in_=pbf(bp)[:, 0:256]),
                 r=[("ps", bp)], w=[("pT", s)])

        def c2_outproj(m, c):
            ch = m * CPM + c
            s = ch % 2
            mp = m % 2
            ts = slice(c * 128, (c + 1) * 128)
            bm = [P.bank(hold=True), P.bank(hold=True)]
            st2[ch] = bm
            for nb in range(2):
                for kc in range(16):
                    if kc < 8:
                        lt = ynTi[s][:, kc, :]
                        rk = [("ynTi", s)]
                    else:
                        lt = yscT[mp][:, kc - 8, ts]
                        rk = [("yscT", mp, kc - 8)]
                    P.op("pe", lambda e, nb=nb, kc=kc, lt=lt: e.matmul(
                        pf(bm[nb]), lhsT=lt, rhs=wo[:, kc, nb * 512:(nb + 1) * 512], start=(kc == 0), stop=(kc == 15)),
                        r=rk + ["wo"], w=[("ps", bm[nb])])
                P.op("act", lambda e, nb=nb, s=s: e.activation(out=junk[:, nb * 512:(nb + 1) * 512], in_=pf(bm[nb]),
                                                          func=AF.Square, accum_out=ss2[s][:, nb:nb + 1]),
                     r=[("ps", bm[nb])], w=["junk", ("ss2", s, nb)])

        def c2_chainA(m, c):
            ch = m * CPM + c
            s = ch % 2
            bm = st2[ch]
            q2 = ss2[s]
            P.op("pool", lambda e: e.tensor_tensor(out=q2[:, 2:3], in0=q2[:, 0:1], in1=q2[:, 1:2], op=ALU.add),
                 r=[("ss2", s, 0), ("ss2", s, 1)], w=[("ss2", s, 2)])
            P.op("pool", lambda e: e.tensor_scalar(out=q2[:, 3:4], in0=q2[:, 2:3], scalar1=1.0 / D, scalar2=EPS,
                                                   op0=ALU.mult, op1=ALU.add), r=[("ss2", s, 2)], w=[("ss2", s, 3)])
            P.op("pool", lambda e: e.tensor_tensor(out=q2[:, 4:5], in0=q2[:, 3:4], in1=neghalf[:, 0:1], op=ALU.pow),
                 r=[("ss2", s, 3), "neghalf"], w=[("ss2", s, 4)])
            x1 = x1s[s]
            x1b = x1bs[s]
            for nb in range(2):
                sl = slice(nb * 512, (nb + 1) * 512)
                P.op("dve", lambda e, nb=nb, sl=sl: e.scalar_tensor_tensor(
                    out=x1[:, sl], in0=pf(bm[nb]), scalar=q2[:, 4:5], in1=npost_b[:, sl], op0=ALU.mult, op1=ALU.mult),
                    r=[("ps", bm[nb]), ("ss2", s, 4), "npost_b"], w=[("x1s", s, nb)])
            P.free(*bm)
            P.op("pool", lambda e: e.tensor_tensor(out=x1[:, :], in0=x1[:, :], in1=xr[s][:, :], op=ALU.add),
                 r=[("x1s", s, 0), ("x1s", s, 1), ("xr", s)], w=[("x1s", s, 0), ("x1s", s, 1)])
            P.op("act", lambda e: e.activation(out=x1b[:, :], in_=x1[:, :], func=AF.Copy),
                 r=[("x1s", s, 0), ("x1s", s, 1)], w=[("x1b", s)])

        def c2_chainA2(m, c):
            ch = m * CPM + c
            s = ch % 2
            x1b = x1bs[s]
            bt = P.bank()
            for kc in range(8):
                P.op("pe", lambda e, kc=kc: e.transpose(out=pbf(bt)[:, kc * 128:(kc + 1) * 128],
                                                        in_=x1b[:, kc * 128:(kc + 1) * 128], identity=ident[:, :]),
                     r=[("x1b", s), "ident"], w=[("ps", bt)])
            P.op("dve", lambda e: e.tensor_copy(out=x1T[s][:, :, :].rearrange("p k t -> p (k t)"), in_=pbf(bt)),
                 r=[("ps", bt)], w=[("x1T", s)])

        def c2_gate(m, c):
            ch = m * CPM + c
            s = ch % 2
            t0 = ch * 128
            x1 = x1s[s]
            bg = [P.bank(), P.bank()]
            bq = [P.bank(), P.bank()]
            for nb in range(2):
                for kc in range(8):
                    P.op("pe", lambda e, nb=nb, kc=kc: e.matmul(
                        pf(bg[nb]), lhsT=x1T[s][:, kc, :], rhs=wg[:, kc, nb * 512:(nb + 1) * 512],
                        start=(kc == 0), stop=(kc == 7)), r=[("x1T", s), "wg"], w=[("ps", bg[nb])])
                for kc in range(2):
                    P.op("pe", lambda e, nb=nb, kc=kc: e.matmul(
                        pf(bq[nb]), lhsT=pT[s][:, kc, :], rhs=wp[:, kc, nb * 512:(nb + 1) * 512],
                        start=(kc == 0), stop=(kc == 1)), r=[("pT", s), "wp"], w=[("ps", bq[nb])])
            for nb in range(2):
                sl = slice(nb * 512, (nb + 1) * 512)
                P.op("act", lambda e, nb=nb, sl=sl: e.activation(out=th2[:, sl], in_=pf(bg[nb]), func=AF.Tanh, scale=0.5),
                     r=[("ps", bg[nb])], w=[("th2", nb)])
                P.op("dve", lambda e, nb=nb, sl=sl: e.scalar_tensor_tensor(
                    out=th2[:, sl], in0=th2[:, sl], scalar=1.0, in1=pf(bq[nb]), op0=ALU.add, op1=ALU.mult),
                    r=[("th2", nb), ("ps", bq[nb])], w=[("th2", nb)])
            P.op("pool", lambda e: e.tensor_tensor(out=x1[:, :], in0=th2[:, :], in1=x1[:, :], op=ALU.add),
                 r=[("th2", 0), ("th2", 1), ("x1s", s, 0), ("x1s", s, 1)], w=[("x1s", s, 0), ("x1s", s, 1)])
            P.dma(lambda e: e.dma_start(out=xdst[t0:t0 + 128, :], in_=x1[:, :]),
                  r=[("x1s", s, 0), ("x1s", s, 1)], w=[("xdst", l, ch)], sem="xo%d" % s)

        def b2_halo():
            P.op("dve", lambda e: e.tensor_copy(out=chb[:, :, 0:2], in_=chb[:, :, MT2:MT2 + 2]),
                 r=[("chb", j) for j in range(8)], w=[("chb", j) for j in range(8)])

        def b2_proj(m, j):
            mp = m % 2
            tbs = tb[j % 3]

            def proj(off):
                b = P.bank()
                for kc in range(8):
                    P.op("pe", lambda e, b=b, kc=kc: e.matmul(
                        pf(b)[:, 0:MT2], lhsT=w2[:, kc, off + j * 128:off + (j + 1) * 128], rhs=hT[mp][:, kc, :],
                        start=(kc == 0), stop=(kc == 7)), r=["w2"] + hT_keys(mp), w=[("ps", b)])
                return b
            b = proj(0)
            P.op("act", lambda e, b=b: e.activation(out=hs[:, :], in_=pf(b)[:, 0:MT2], func=AF.Copy), r=[("ps", b)], w=["hs"])
            b = proj(2048)
            P.op("dve", lambda e, b=b: e.tensor_tensor(out=chb[:, j, 2:MT2 + 2], in0=pf(b)[:, 0:MT2], in1=hs[:, :], op=ALU.mult),
                 r=[("ps", b), "hs"], w=[("chb", j)])
            b = proj(3072)
            P.op("act", lambda e, b=b: e.activation(out=zs[:, :], in_=pf(b)[:, 0:MT2], func=AF.Silu), r=[("ps", b)], w=["zs"])
            b = proj(1024)
            P.op("dve", lambda e, b=b: e.tensor_tensor(out=tbs[:, :], in0=pf(b)[:, 0:MT2], in1=zs[:, :], op=ALU.mult),
                 r=[("ps", b), "zs"], w=[("tb", j % 3)])

        def b2_conv(m, j):
            mp = m % 2
            tbs = tb[j % 3]
            b = P.bank()
            for k in range(3):
                P.op("pe", lambda e, b=b, k=k: e.matmul(
                    pf(b)[:, 0:MT2], lhsT=dgs[:, j * 3 + k, :], rhs=chb[:, j, k:k + MT2], start=(k == 0), stop=(k == 2)),
                    r=["dgs", ("chb", j)], w=[("ps", b)])
            P.op("dve", lambda e, b=b: e.tensor_tensor(out=yscT[mp][:, j, :], in0=pf(b)[:, 0:MT2], in1=tbs[:, :], op=ALU.mult),
                 r=[("ps", b), ("tb", j % 3)], w=[("yscT", mp, j)])

        def b2_step(m, k):
            if k == 0:
                b2_halo()
            if k < 8:
                b2_proj(m, k)
            if 2 <= k < 10:
                b2_conv(m, k - 2)

        def hT_load(m2):
            p2 = m2 % 2
            half = m2 % 2
            srcv = hT_d[m2 // 2].rearrange("p (k t) -> p k t", k=8)[:, :, half * MT2:(half + 1) * MT2]
            P.dma(lambda e: e.dma_start(out=hT[p2][:, :, :], in_=srcv),
                  r=[("hT_d", m2 // 2)], w=[("hT", p2, c) for c in range(CPM)], sem="hTi%d" % p2)

        hT_load(0)
        for k in range(10):
            b2_step(0, k)
        if NM2 > 1:
            hT_load(1)
        c2_loads(0)
        for m in range(NM2):
            ch0 = m * CPM
            nxt = m + 1 < NM2
            pa = m + 2 < NM2
            tq = list(range(10)) if nxt else []

            def T(n):
                for _ in range(n):
                    if tq:
                        b2_step(m + 1, tq.pop(0))
            if pa:
                hT_load(m + 2)
            if ch0 + 1 < NCH:
                c2_loads(ch0 + 1)
            c2_ptrans(ch0)
            c2_outproj(m, 0)
            c2_chainA(m, 0)
            T(2)
            c2_chainA2(m, 0)
            if ch0 + 2 < NCH:
                c2_loads(ch0 + 2)
            c2_ptrans(ch0 + 1)
            c2_outproj(m, 1)
            c2_chainA(m, 1)
            T(1)
            c2_gate(m, 0)
            T(2)
            c2_chainA2(m, 1)
            T(2)
            c2_gate(m, 1)
            T(10)

    for l in range(L):
        xsrc = x_in if l == 0 else x1_d
        xdst = out_d if l == L - 1 else x1_d
        pass1(l, xsrc)
        pass2(l, xsrc, xdst)

    P.finalize(nc, stack)
    stack.close()
    return nc, P


def _host_inputs(S, inp, b):
    f = np.float32
    L = inp["w_in"].shape[0]
    bf = ml_dtypes.bfloat16
    k = np.arange(128)
    tri = (k[:, None] <= k[None, :]).astype(f)
    triu = (k[:, None] > k[None, :]).astype(f)
    negm = np.tile(np.where(k[None, :] < k[:, None], NEG, 0.0).astype(f), (1, 4)).astype(bf)
    m = {
        "x": np.ascontiguousarray(inp["x"][b, :S], dtype=f),
        "p": np.ascontiguousarray(inp["p"][:, b, :S], dtype=f),
        "w_in": np.ascontiguousarray(inp["w_in"], dtype=f),
        "w_out": np.ascontiguousarray(inp["w_out"], dtype=f),
        "w_gate": np.ascontiguousarray(inp["w_ple_gate"], dtype=f),
        "w_ple": np.ascontiguousarray(inp["w_ple_proj"], dtype=f),
        "npre": np.ascontiguousarray(inp["norm_pre"].reshape(L, 8, 128).transpose(0, 2, 1), dtype=f),
        "snorm": np.ascontiguousarray(inp["ssd_norm"].reshape(L, 8, 128).transpose(0, 2, 1), dtype=f),
        "npost": np.ascontiguousarray(inp["norm_post"], dtype=f),
        "cw": np.ascontiguousarray(inp["ssd_conv_w"].reshape(L, 4, 10, 128).transpose(0, 3, 2, 1).reshape(L, 128, 40), dtype=f),
        "cb": np.ascontiguousarray(inp["ssd_conv_b"].reshape(L, 10, 128).transpose(0, 2, 1), dtype=f),
        "scw": np.ascontiguousarray(inp["sc_conv_w"].reshape(L, 3, 8, 128).transpose(0, 3, 2, 1).reshape(L, 128, 24), dtype=f),
        "dsk": np.ascontiguousarray(np.repeat(inp["d_skip"], 64, axis=1).reshape(L, 8, 128).transpose(0, 2, 1), dtype=f),
        "dtb": np.ascontiguousarray(inp["dt_bias"], dtype=f),
        "alog": np.ascontiguousarray(inp["a_log"], dtype=f),
        "ident": np.eye(128, dtype=f).astype(bf),
        "tri": tri, "triu": triu.astype(bf), "negm": negm, "ones": np.ones((128, 128), f),
    }
    return m


_CACHE = {}


def run(inp, S, cores, trace=False):
    inp = {k: np.asarray(v) for k, v in inp.items()}
    L = inp["w_in"].shape[0]
    key = (S, L)
    if key not in _CACHE:
        _CACHE[key] = build_program(S, L)[0]
    nc = _CACHE[key]
    in_maps = [_host_inputs(S, inp, b) for b in range(cores)]
    res = run_bass_kernel_spmd(nc, in_maps, core_ids=list(range(cores)))
    return np.stack([np.asarray(r["out"], dtype=np.float32) for r in res.results], axis=0)


def kernel(**inputs):
    x = np.asarray(inputs["x"])
    B, S, _ = x.shape
    return run(inputs, S, B)
```
